# Optimizing a Trainium2 kernel written in Bass

```python
import math
import jax, jax.numpy as jnp
from jax import lax
import numpy as np

D_MODEL = 2048
BATCH = 2
SEQ = 4096
DEPTH = 2
DEC_BATCH = 8
DEC_SEQ = 4096
PAST_LEN = 128

N_MIXERS = 2
N_HYENA_LAYERS = (DEPTH + N_MIXERS - 1) // N_MIXERS
N_ATTN_LAYERS = DEPTH // N_MIXERS

HY_WIDTH = D_MODEL
HY_ORDER = 2
HY_DIRS = 2
SHORT_CONV = 3
POS_EMB_DIM = 33
POS_BANDS = (POS_EMB_DIM - 1) // 2
FILTER_HIDDEN = 64
DECAY_TARGET = 1e-2
FAST_DECAY_PCT = 0.3
SLOW_DECAY_PCT = 1.5
MIN_DECAY = math.log(DECAY_TARGET) / SLOW_DECAY_PCT
MAX_DECAY = math.log(DECAY_TARGET) / FAST_DECAY_PCT

N_HEADS = 16
N_KV_HEADS = 4
HEAD_DIM = 128
GROUP = N_HEADS // N_KV_HEADS
ATTN_WIDTH = N_HEADS * HEAD_DIM
KV_WIDTH = N_KV_HEADS * HEAD_DIM
WINDOW = 128
BLOCK = 128

LN_EPS = 1e-5
DEEPNORM_ALPHA = (2 * DEPTH) ** 0.25
DEEPNORM_BETA = (8 * DEPTH) ** -0.25

kernel_name = "hyena_swa_alibi_deepnorm_encoder"

F32 = jnp.float32


def layer_norm(x, g, b):
    xf = x.astype(F32)
    mu = jnp.mean(xf, -1, keepdims=True)
    var = jnp.mean(jnp.square(xf - mu), -1, keepdims=True)
    y = (xf - mu) * lax.rsqrt(var + LN_EPS) * g.astype(F32) + b.astype(F32)
    return y.astype(x.dtype)


def centred_short_conv(u, w, b):
    L = u.shape[1]
    pad = SHORT_CONV // 2
    up = jnp.pad(u, ((0, 0), (pad, pad), (0, 0)))
    out = b
    for j in range(SHORT_CONV):
        out = out + up[:, j:j + L] * w[j]
    return out


def hyena_filters(L, w_f1, b_f1, fr1, w_f2, b_f2, fr2, w_f3, b_f3, fr3, w_f4):
    t_norm = jnp.linspace(0.0, 1.0, L, dtype=F32)
    w = 2.0 * math.pi * jnp.arange(L, dtype=F32) / L
    f = jnp.linspace(1e-4, POS_BANDS - 1, POS_BANDS, dtype=F32)
    ang = w[:, None] * f[None, :]
    feat = jnp.concatenate([t_norm[:, None], jnp.cos(ang), -jnp.sin(ang)], axis=-1)
    h = jnp.sin(fr1.astype(F32) * (feat @ w_f1.astype(F32) + b_f1.astype(F32)))
    h = jnp.sin(fr2.astype(F32) * (h @ w_f2.astype(F32) + b_f2.astype(F32)))
    h = jnp.sin(fr3.astype(F32) * (h @ w_f3.astype(F32) + b_f3.astype(F32)))
    h = (h @ w_f4.astype(F32)).reshape(L, HY_ORDER, HY_DIRS, HY_WIDTH)
    deltas = jnp.abs(jnp.linspace(MIN_DECAY, MAX_DECAY, HY_WIDTH, dtype=F32))
    decay = jnp.exp(-t_norm[:, None] * deltas[None, :])
    h = h * decay[:, None, None, :]
    k = jnp.concatenate([h[:, :, 0],
                         jnp.zeros((1, HY_ORDER, HY_WIDTH), F32),
                         h[:0:-1, :, 1]], axis=0)
    k = k * lax.rsqrt(jnp.sum(jnp.square(k), axis=0, keepdims=True) + 1e-12)
    return jnp.fft.rfft(k, axis=0)


def long_conv(z, kf):
    L = z.shape[1]
    Z = jnp.fft.rfft(z, n=2 * L, axis=1)
    return jnp.fft.irfft(Z * kf[None], n=2 * L, axis=1)[:, :L]


def hyena_mixer(x, w_in, b_in, w_sc, b_sc, w_f1, b_f1, fr1, w_f2, b_f2, fr2,
                w_f3, b_f3, fr3, w_f4, h_bias, w_out, b_out):
    L = x.shape[1]
    E = HY_WIDTH
    proj = x @ w_in + b_in
    u = centred_short_conv(proj[..., :3 * E], w_sc, b_sc)
    g = proj[..., 3 * E:]
    v, x1, x2 = jnp.split(u, 3, axis=-1)
    kf = hyena_filters(L, w_f1, b_f1, fr1, w_f2, b_f2, fr2, w_f3, b_f3, fr3, w_f4)
    z = v.astype(F32)
    for n, gate in enumerate((x1, x2)):
        z = gate.astype(F32) * (long_conv(z, kf[:, n]) + h_bias[n].astype(F32) * z)
    y = z.astype(x.dtype) * jax.nn.silu(g)
    return y @ w_out + b_out


def alibi_slopes():
    return jnp.exp2(-8.0 * jnp.arange(1, N_HEADS + 1, dtype=F32) / N_HEADS)


def banded_attention(q, k, v, sink):
    B, L = q.shape[0], q.shape[1]
    nb = L // BLOCK
    qb = q.reshape(B, nb, BLOCK, N_KV_HEADS, GROUP, HEAD_DIM)

    def windows(t):
        tp = jnp.pad(t, ((0, 0), (BLOCK, BLOCK), (0, 0), (0, 0)))
        tp = tp.reshape(B, nb + 2, BLOCK, N_KV_HEADS, HEAD_DIM)
        return jnp.concatenate([tp[:, :-2], tp[:, 1:-1], tp[:, 2:]], axis=2)

    kw, vw = windows(k), windows(v)
    scores = jnp.einsum('bnqkgd,bnskd->bnkgqs', qb, kw,
                        preferred_element_type=F32) * (HEAD_DIM ** -0.5)
    qi = jnp.arange(BLOCK)[:, None]
    kj = jnp.arange(3 * BLOCK)[None, :]
    dist = jnp.abs(qi - kj + BLOCK)
    key_pos = jnp.arange(nb)[:, None] * BLOCK - BLOCK + jnp.arange(3 * BLOCK)[None, :]
    valid = (dist <= WINDOW)[None] & ((key_pos >= 0) & (key_pos < L))[:, None, :]
    slopes = alibi_slopes().reshape(N_KV_HEADS, GROUP)
    logits = scores - slopes[:, :, None, None] * dist.astype(F32)
    logits = jnp.where(valid[None, :, None, None], logits, -jnp.inf)
    s = sink.astype(F32).reshape(N_KV_HEADS, GROUP)[:, :, None, None]
    m = jnp.maximum(jnp.max(logits, axis=-1, keepdims=True), s)
    p = jnp.exp(logits - m)
    denom = jnp.sum(p, axis=-1, keepdims=True) + jnp.exp(s - m)
    p = (p / denom).astype(v.dtype)
    o = jnp.einsum('bnkgqs,bnskd->bnqkgd', p, vw)
    return o.reshape(B, L, ATTN_WIDTH)


def attention_mixer(x, w_in, sink, w_out):
    B, L, _ = x.shape
    proj = x @ w_in
    q = proj[..., :ATTN_WIDTH].reshape(B, L, N_KV_HEADS, GROUP, HEAD_DIM)
    k = proj[..., ATTN_WIDTH:ATTN_WIDTH + KV_WIDTH].reshape(B, L, N_KV_HEADS, HEAD_DIM)
    v = proj[..., ATTN_WIDTH + KV_WIDTH:ATTN_WIDTH + 2 * KV_WIDTH].reshape(B, L, N_KV_HEADS, HEAD_DIM)
    g = proj[..., ATTN_WIDTH + 2 * KV_WIDTH:]
    o = banded_attention(q, k, v, sink)
    return (o * jax.nn.silu(g)) @ w_out


def trunk(x, ln_g, ln_b, hy_w_in, hy_b_in, hy_w_sc, hy_b_sc, hy_w_f1, hy_b_f1, hy_fr1,
          hy_w_f2, hy_b_f2, hy_fr2, hy_w_f3, hy_b_f3, hy_fr3, hy_w_f4, hy_h_bias,
          hy_w_out, hy_b_out, at_w_in, at_sink, at_w_out):
    for i in range(DEPTH):
        j = i // N_MIXERS
        if i % N_MIXERS == 0:
            h = hyena_mixer(x, hy_w_in[j], hy_b_in[j], hy_w_sc[j], hy_b_sc[j],
                            hy_w_f1[j], hy_b_f1[j], hy_fr1[j], hy_w_f2[j], hy_b_f2[j], hy_fr2[j],
                            hy_w_f3[j], hy_b_f3[j], hy_fr3[j], hy_w_f4[j], hy_h_bias[j],
                            hy_w_out[j], hy_b_out[j])
        else:
            h = attention_mixer(x, at_w_in[j], at_sink[j], at_w_out[j])
        x = layer_norm(DEEPNORM_ALPHA * x + h.astype(x.dtype), ln_g[i], ln_b[i])
    return x


def setup_inputs(seed: int = 0) -> dict:
    key = jax.random.key(seed)
    ks = jax.random.split(key, 24)
    E = HY_WIDTH
    NA, NB = N_HYENA_LAYERS, N_ATTN_LAYERS

    def nrm(k, shape, scale):
        return scale * jax.random.normal(k, shape, F32)

    return {
        "x_prompt": nrm(ks[0], (BATCH, SEQ, D_MODEL), 1.0),
        "x_sample": nrm(ks[1], (DEC_BATCH, DEC_SEQ, D_MODEL), 1.0),
        "ln_g": 1.0 + nrm(ks[2], (DEPTH, D_MODEL), 0.02),
        "ln_b": nrm(ks[3], (DEPTH, D_MODEL), 0.02),
        "hy_w_in": nrm(ks[4], (NA, D_MODEL, 4 * E), D_MODEL ** -0.5),
        "hy_b_in": nrm(ks[5], (NA, 4 * E), 0.02),
        "hy_w_sc": nrm(ks[6], (NA, SHORT_CONV, 3 * E), SHORT_CONV ** -0.5),
        "hy_b_sc": nrm(ks[7], (NA, 3 * E), 0.02),
        "hy_w_f1": nrm(ks[8], (NA, POS_EMB_DIM, FILTER_HIDDEN), POS_EMB_DIM ** -0.5),
        "hy_b_f1": nrm(ks[9], (NA, FILTER_HIDDEN), 0.1),
        "hy_fr1": 1.0 + nrm(ks[10], (NA, FILTER_HIDDEN), 0.1),
        "hy_w_f2": nrm(ks[11], (NA, FILTER_HIDDEN, FILTER_HIDDEN), FILTER_HIDDEN ** -0.5),
        "hy_b_f2": nrm(ks[12], (NA, FILTER_HIDDEN), 0.1),
        "hy_fr2": 1.0 + nrm(ks[13], (NA, FILTER_HIDDEN), 0.1),
        "hy_w_f3": nrm(ks[14], (NA, FILTER_HIDDEN, FILTER_HIDDEN), FILTER_HIDDEN ** -0.5),
        "hy_b_f3": nrm(ks[15], (NA, FILTER_HIDDEN), 0.1),
        "hy_fr3": 1.0 + nrm(ks[16], (NA, FILTER_HIDDEN), 0.1),
        "hy_w_f4": nrm(ks[17], (NA, FILTER_HIDDEN, HY_ORDER * HY_DIRS * E), FILTER_HIDDEN ** -0.5),
        "hy_h_bias": nrm(ks[18], (NA, HY_ORDER, E), 0.5),
        "hy_w_out": nrm(ks[19], (NA, E, D_MODEL), DEEPNORM_BETA * E ** -0.5),
        "hy_b_out": nrm(ks[20], (NA, D_MODEL), 0.02),
        "at_w_in": nrm(ks[21], (NB, D_MODEL, 2 * ATTN_WIDTH + 2 * KV_WIDTH), D_MODEL ** -0.5),
        "at_sink": nrm(ks[22], (NB, N_HEADS), 0.5),
        "at_w_out": nrm(ks[23], (NB, ATTN_WIDTH, D_MODEL), DEEPNORM_BETA * ATTN_WIDTH ** -0.5),
    }


def reference(x_prompt, x_sample, ln_g, ln_b, hy_w_in, hy_b_in, hy_w_sc, hy_b_sc,
              hy_w_f1, hy_b_f1, hy_fr1, hy_w_f2, hy_b_f2, hy_fr2, hy_w_f3, hy_b_f3, hy_fr3,
              hy_w_f4, hy_h_bias, hy_w_out, hy_b_out, at_w_in, at_sink, at_w_out):
    y_prompt = trunk(x_prompt, ln_g, ln_b, hy_w_in, hy_b_in, hy_w_sc, hy_b_sc,
                     hy_w_f1, hy_b_f1, hy_fr1, hy_w_f2, hy_b_f2, hy_fr2, hy_w_f3, hy_b_f3, hy_fr3,
                     hy_w_f4, hy_h_bias, hy_w_out, hy_b_out, at_w_in, at_sink, at_w_out)
    y_sample = trunk(x_sample, ln_g, ln_b, hy_w_in, hy_b_in, hy_w_sc, hy_b_sc,
                     hy_w_f1, hy_b_f1, hy_fr1, hy_w_f2, hy_b_f2, hy_fr2, hy_w_f3, hy_b_f3, hy_fr3,
                     hy_w_f4, hy_h_bias, hy_w_out, hy_b_out, at_w_in, at_sink, at_w_out)
    return (y_prompt, y_sample)
```

```python
import numpy as np
import concourse.bass as bass
import concourse.mybir as mybir
from concourse.bass_utils import run_bass_kernel_spmd

F32 = mybir.dt.float32
BF16 = mybir.dt.bfloat16
ALU = mybir.AluOpType
AF = mybir.ActivationFunctionType
AX = mybir.AxisListType

SEM_EPOCH = 24000


class Sched:
    def __init__(self, nc, ctx, dma_sems=6):
        self.nc = nc
        self.ctx = ctx
        self.ops = []
        self.last_w = {}
        self.readers = {}
        self.dma_sems = dma_sems

    def add(self, eng, fn, reads=(), writes=(), dma=False):
        i = len(self.ops)
        deps = set()
        for r in reads:
            w = self.last_w.get(r)
            if w is not None:
                deps.add(w)
        for r in writes:
            w = self.last_w.get(r)
            if w is not None:
                deps.add(w)
            deps.update(self.readers.get(r, ()))
        for r in reads:
            self.readers.setdefault(r, []).append(i)
        for r in writes:
            self.last_w[r] = i
            self.readers[r] = []
        deps.discard(i)
        self.ops.append(dict(eng=eng, fn=fn, deps=deps, dma=dma, sig=None, need=False))
        return i

    def pe(self, fn, reads=(), writes=()):
        return self.add('pe', fn, reads, writes)

    def act(self, fn, reads=(), writes=()):
        return self.add('act', fn, reads, writes)

    def dve(self, fn, reads=(), writes=()):
        return self.add('dve', fn, reads, writes)

    def pool(self, fn, reads=(), writes=()):
        return self.add('pool', fn, reads, writes)

    def dma(self, q, out, in_, reads=(), writes=()):
        return self.add(q, lambda e: e.dma_start(out=out, in_=in_), reads, writes, dma=True)

    def emit(self):
        nc = self.nc
        ops = self.ops
        for op in ops:
            for d in op['deps']:
                a = ops[d]
                if a['eng'] == 'pe' and op['eng'] == 'pe' and not a['dma']:
                    continue
                a['need'] = True
        last_of = {}
        for idx, op in enumerate(ops):
            if not op['dma']:
                last_of[op['eng']] = idx
        for idx in last_of.values():
            ops[idx]['need'] = True

        def new_sem(name):
            h = nc.alloc_semaphore(name=name)
            return h

        ccount = self.ctx.ccount
        dcount = self.ctx.dcount
        dsem = self.ctx.dsem
        for idx, op in enumerate(ops):
            e = op['eng']
            if op['dma']:
                j = dcount.get(e, 0)
                dcount[e] = j + 1
                slot = j % self.dma_sems
                key = (e, slot)
                if key not in dsem or dsem[key][1] + 16 > SEM_EPOCH:
                    dsem[key] = [new_sem("d_%s_%d_%d" % (e, slot, j)), 0]
                    op['prev'] = None
                else:
                    op['prev'] = (dsem[key][0], dsem[key][1])
                dsem[key][1] += 16
                op['sig'] = (dsem[key][0], dsem[key][1])
            elif op['need']:
                if e not in ccount or ccount[e][1] + 1 > SEM_EPOCH:
                    ccount[e] = [new_sem("c_%s_%d" % (e, idx)), 0]
                ccount[e][1] += 1
                op['sig'] = (ccount[e][0], ccount[e][1])
        per_eng = {}
        for idx, op in enumerate(ops):
            per_eng.setdefault(op['eng'], []).append(idx)

        final_waits = [op['sig'] for op in ops if op['dma']]
        final_waits += [ops[idx]['sig'] for idx in last_of.values()]

        def run_engine(ename, eh, is_last_waiter=True):
            observed = {}
            for idx in per_eng.get(ename, []):
                op = ops[idx]
                waits = {}
                for d in op['deps']:
                    a = ops[d]
                    if a['sig'] is None:
                        continue
                    if a['eng'] == 'pe' and ename == 'pe' and not a['dma']:
                        continue
                    s, v = a['sig']
                    k = id(s)
                    if k not in waits or waits[k][1] < v:
                        waits[k] = (s, v)
                if op['dma'] and op.get('prev') is not None:
                    s, v = op['prev']
                    k = id(s)
                    if k not in waits or waits[k][1] < v:
                        waits[k] = (s, v)
                for k, (s, v) in waits.items():
                    if observed.get(k, 0) >= v:
                        continue
                    observed[k] = v
                    eh.wait_ge(s, v)
                ins = op['fn'](eh)
                if op['sig'] is not None:
                    s, v = op['sig']
                    ins.then_inc(s, 16 if op['dma'] else 1)
            if is_last_waiter:
                last = {}
                for s, v in final_waits:
                    k = id(s)
                    if k not in last or last[k][1] < v:
                        last[k] = (s, v)
                for k, (s, v) in last.items():
                    if observed.get(k, 0) >= v:
                        continue
                    eh.wait_ge(s, v)

        with nc.Block() as block:
            @block.tensor
            def _(e):
                run_engine('pe', e)

            @block.scalar
            def _(e):
                run_engine('act', e)

            @block.vector
            def _(e):
                run_engine('dve', e)

            @block.gpsimd
            def _(e):
                run_engine('pool', e)

            @block.sync
            def _(e):
                run_engine('sp', e)


class SemCtx:
    def __init__(self):
        self.ccount = {}
        self.dcount = {}
        self.dsem = {}


import math
from contextlib import ExitStack
import ml_dtypes

L = 4096
D = 2048
NFFT = 8192
ALPHA = (2 * 2) ** 0.25
LN_EPS = 1e-5
TWO_PI = 2.0 * math.pi
MAGIC = 12582912.0
SCALE = 128 ** -0.5


def _consts():
    c = {}
    c["ident_f"] = np.eye(128, dtype=np.float32)
    c["ident_b"] = np.eye(128).astype(ml_dtypes.bfloat16)
    t_norm = np.linspace(0.0, 1.0, L, dtype=np.float32)
    w = (2.0 * math.pi * np.arange(L, dtype=np.float32) / L).astype(np.float32)
    f = np.linspace(1e-4, 15, 16, dtype=np.float32)
    ang = w[:, None] * f[None, :]
    feat = np.concatenate([t_norm[:, None], np.cos(ang), -np.sin(ang)], axis=-1).astype(np.float32)
    c["featT"] = np.ascontiguousarray(feat.T)
    c["tnorm"] = np.ascontiguousarray(t_norm.reshape(32, 128).T)
    min_decay = math.log(1e-2) / 1.5
    max_decay = math.log(1e-2) / 0.3
    deltas = np.abs(np.linspace(min_decay, max_decay, D, dtype=np.float32))
    c["negdelta"] = (-deltas).astype(np.float32)
    n1 = np.arange(32)[:, None, None]
    n2 = np.arange(128)[None, :, None]
    k1 = np.arange(64)[None, None, :]
    n1 = np.arange(64)[:, None, None]
    th = 2 * np.pi * ((128 * n1 + n2) * (k1 + 0.5) % NFFT) / NFFT
    F1 = np.concatenate([np.cos(th), -np.sin(th)], axis=2)
    F1[32:] *= -1.0
    c["F1c"] = F1.astype(ml_dtypes.bfloat16)
    c["Jrev"] = np.eye(128)[::-1].copy().astype(ml_dtypes.bfloat16)
    nn2 = np.arange(128)[:, None]
    k2 = np.arange(64)[None, :]
    ph = 2 * np.pi * ((nn2 * k2) % 128) / 128
    Cr, Ci = np.cos(ph), -np.sin(ph)
    c["Cmat"] = np.stack([Cr, Ci, -Ci, -Cr], axis=1).astype(ml_dtypes.bfloat16)
    ph2 = ph.T
    Er, Ei = 2 * np.cos(ph2), 2 * np.sin(ph2)
    E = np.stack([Er, Ei, -Ei], axis=1)
    c["Emat"] = np.concatenate([E, E], axis=0).astype(ml_dtypes.bfloat16)
    kk = np.arange(64)[:, None, None]
    m2 = np.arange(128)[None, :, None]
    m1 = np.arange(32)[None, None, :]
    th2 = 2 * np.pi * ((128 * m1 + m2) * (kk + 0.5) % NFFT) / NFFT
    T = np.concatenate([np.cos(th2), -np.sin(th2)], axis=0) / NFFT
    c["Tmat"] = T.astype(ml_dtypes.bfloat16)
    slopes = np.exp2(-8.0 * np.arange(1, 17, dtype=np.float32) / 16)
    qi = np.arange(128)[:, None]
    kj = np.arange(384)[None, :]
    dist = np.abs(qi - kj + 128).astype(np.float32)
    ab = -slopes[None, :, None] * dist[:, None, :]
    ab = np.where((dist <= 128)[:, None, :], ab, -30000.0)
    c["abias"] = np.ascontiguousarray(ab.astype(np.float32))
    return c


class K:
    pass


def skew(N, stages):
    K_ = len(stages)
    for t in range(N + K_ - 1):
        for k in reversed(range(K_)):
            n = t - k
            if 0 <= n < N:
                stages[k](n)


_UID = [0]


def _alloc(es, nc, name, shape, dt):
    _UID[0] += 1
    return es.enter_context(nc.sbuf_tensor("%s_%d" % (name, _UID[0]), shape, dt))


def _palloc(es, nc, name, shape, dt):
    _UID[0] += 1
    return es.enter_context(nc.psum_tensor("%s_%d" % (name, _UID[0]), shape, dt))


def ph_cast(nc, ctx, pairs):
    with ExitStack() as es:
        cs = _alloc(es, nc, "cs", [128, 4, 2048], F32)
        cb = _alloc(es, nc, "cb", [128, 4, 2048], BF16)
        S = Sched(nc, ctx)
        jobs = []
        for src, dst, R, C in pairs:
            for r in range(R // 128):
                for c0 in range(0, C, 2048):
                    jobs.append((src, dst, r, c0, min(2048, C - c0)))

        def st_l(n):
            src, dst, r, c0, wd = jobs[n]
            S.dma('sp', cs[:, n % 4, :wd], src[r * 128:(r + 1) * 128, c0:c0 + wd], writes=[('cs', n % 4)])

        def st_c(n):
            src, dst, r, c0, wd = jobs[n]
            k = n % 4
            eng = ('dve', 'act', 'pool')[n % 3]
            if eng == 'act':
                S.act(lambda e: e.copy(cb[:, k, :wd], cs[:, k, :wd]), [('cs', k)], [('cb', k)])
            else:
                S.add(eng, lambda e: e.tensor_copy(cb[:, k, :wd], cs[:, k, :wd]), [('cs', k)], [('cb', k)])

        def st_s(n):
            src, dst, r, c0, wd = jobs[n]
            S.dma('act', dst[r * 128:(r + 1) * 128, c0:c0 + wd], cb[:, n % 4, :wd], reads=[('cb', n % 4)])

        skew(len(jobs), [st_l, st_c, st_s])
        S.emit()


def ph_zero_rows(nc, ctx, g):
    with ExitStack() as es:
        z = _alloc(es, nc, "zr", [2, 8192], F32)
        S = Sched(nc, ctx)
        S.dve(lambda e: e.memset(z[:], 0.0), writes=['z'])
        S.dma('sp', g.Ptm[0:1, :], z[0:1, :], reads=['z'])
        S.dma('sp', g.Ptm[4097:4098, :], z[1:2, :], reads=['z'])
        zb16 = _alloc(es, nc, "zb16", [2, D], BF16)
        S.dve(lambda e: e.memset(zb16[:], 0.0), writes=['zb16'])
        for o in range(2):
            S.dma('sp', g.Gt[o, 4096:4097, :], zb16[o:o + 1, :], reads=['zb16'])
        S.emit()


def ph_filter_mlp(nc, ctx, g):
    with ExitStack() as es:
        ft = _alloc(es, nc, "ft", [33, L], F32)
        w1 = _alloc(es, nc, "w1", [33, 64], F32)
        w23 = _alloc(es, nc, "w23", [64, 2, 64], F32)
        vec = _alloc(es, nc, "vec", [64, 9], F32)
        hA = _alloc(es, nc, "hA", [64, L], F32)
        hB = _alloc(es, nc, "hB", [64, L], F32)
        ta = _alloc(es, nc, "ta", [64, 2, 512], F32)
        tk = _alloc(es, nc, "tk", [64, 2, 512], F32)
        ps = _palloc(es, nc, "ps", [128, 4, 512], F32)
        S = Sched(nc, ctx)
        S.dma('sp', ft[:], g.featT, writes=['ft'])
        S.dma('sp', w1[:], g.w_f1, writes=['w1'])
        S.dma('sp', w23[:, 0, :], g.w_f2, writes=['w23'])
        S.dma('sp', w23[:, 1, :], g.w_f3, writes=['w23'])
        for j, v in enumerate([g.b_f1, g.fr1, g.b_f2, g.fr2, g.b_f3, g.fr3]):
            S.dma('sp', vec[:, j:j + 1], v.rearrange("(p o) -> p o", o=1), writes=['vec'])
        for l in range(3):
            S.dve(lambda e, l=l: e.tensor_tensor(vec[:, 6 + l:7 + l], vec[:, 2 * l:2 * l + 1], vec[:, 2 * l + 1:2 * l + 2], ALU.mult),
                  ['vec'], ['vec'])
        srcs = [ft, hA, hB]
        dsts = [hA, hB, hA]
        names = ['ft', 'hA', 'hB', 'hA']
        for l in range(3):
            src, dst = srcs[l], dsts[l]
            W = w1[:, :] if l == 0 else w23[:, l - 1, :]
            for c in range(8):
                b = c % 4
                s2 = c % 2
                S.pe(lambda e, b=b, W=W, src=src, c=c: e.matmul(ps[0:64, b, :], W, src[:, c * 512:(c + 1) * 512], start=True, stop=True),
                     [names[l], 'w1', 'w23'], [('ps', b)])
                S.dve(lambda e, b=b, s2=s2, l=l: e.tensor_scalar(ta[:, s2, :], ps[0:64, b, :], vec[:, 2 * l + 1:2 * l + 2], vec[:, 6 + l:7 + l], ALU.mult, ALU.add),
                      [('ps', b), 'vec'], [('ta', s2)])
                S.dve(lambda e, s2=s2: e.tensor_scalar(tk[:, s2, :], ta[:, s2, :], 1.0 / TWO_PI, MAGIC, ALU.mult, ALU.add), [('ta', s2)], [('tk', s2)])
                S.dve(lambda e, s2=s2: e.tensor_single_scalar(tk[:, s2, :], tk[:, s2, :], MAGIC, ALU.subtract), [('tk', s2)], [('tk', s2)])
                S.dve(lambda e, s2=s2: e.scalar_tensor_tensor(ta[:, s2, :], tk[:, s2, :], -TWO_PI, ta[:, s2, :], ALU.mult, ALU.add),
                      [('tk', s2), ('ta', s2)], [('ta', s2)])
                S.act(lambda e, s2=s2, dst=dst, c=c: e.activation(dst[:, c * 512:(c + 1) * 512], ta[:, s2, :], AF.Sin), [('ta', s2)], [names[l + 1]])
        S.dma('sp', g.H3, hA[:], reads=['hA'])
        S.emit()


def ph_filter_gen(nc, ctx, g):
    with ExitStack() as es:
        h3 = _alloc(es, nc, "h3", [64, L], F32)
        w4 = _alloc(es, nc, "w4", [64, 8192], F32)
        h3b = _alloc(es, nc, "h3b", [64, L], BF16)
        w4b = _alloc(es, nc, "w4b", [64, 8192], BF16)
        nd = _alloc(es, nc, "nd", [128, D], F32)
        tn = _alloc(es, nc, "tn", [128, 32], F32)
        dec = _alloc(es, nc, "dec", [128, 3, 512], F32)
        kt = _alloc(es, nc, "kt", [128, 4, 512], F32)
        kb = _alloc(es, nc, "kb", [128, 4, 512], BF16)
        sq = _alloc(es, nc, "sq", [128, 4, 512], BF16)
        ones = _alloc(es, nc, "ones", [128, 1], BF16)
        ssr = _alloc(es, nc, "ssr", [1, 8192], F32)
        nr = _alloc(es, nc, "nr", [1, 4096], F32)
        ps = _palloc(es, nc, "ps", [128, 3, 512], F32)
        psr = _palloc(es, nc, "psr", [128, 1, 512], F32)
        pss = _palloc(es, nc, "pss", [128, 4, 512], F32)
        jr = _alloc(es, nc, "jr", [128, 128], BF16)
        kr = _alloc(es, nc, "kr", [128, 2, 512], BF16)
        S = Sched(nc, ctx)
        S.dma('sp', jr[:], g.Jrev, writes=['jr'])
        S.dma('sp', h3[:], g.H3, writes=['h3'])
        S.dma('sp', w4[:], g.w_f4, writes=['w4'])
        S.dma('sp', nd[:], g.negdelta.partition_broadcast(128), writes=['nd'])
        S.dma('sp', tn[:], g.tnorm, writes=['tn'])
        S.dve(lambda e: e.memset(ones[:], 1.0), writes=['ones'])
        S.dve(lambda e: e.tensor_copy(h3b[:], h3[:]), ['h3'], ['h3b'])
        S.act(lambda e: e.copy(w4b[:, 0:4096], w4[:, 0:4096]), ['w4'], ['w4b0'])
        S.pool(lambda e: e.tensor_copy(w4b[:, 4096:8192], w4[:, 4096:8192]), ['w4'], ['w4b1'])

        def dec_n(n):
            cc, r = divmod(n, 128)
            i, od = divmod(r, 4)
            return cc, i, od, (n // 4) % 3, n % 4

        def st0(n):
            cc, i, od, ds, k = dec_n(n)
            if od == 0:
                S.act(lambda e: e.activation(dec[:, ds, :], nd[:, cc * 512:(cc + 1) * 512], AF.Exp, scale=tn[:, i:i + 1]), ['nd', 'tn'], [('dec', ds)])
            col0 = od * 2048 + cc * 512
            S.pe(lambda e: e.matmul(ps[:, n % 3, :], h3b[:, i * 128:(i + 1) * 128], w4b[:, col0:col0 + 512], start=True, stop=True),
                 ['h3b', 'w4b0', 'w4b1'], [('ps', n % 3)])

        def st1(n):
            cc, i, od, ds, k = dec_n(n)
            S.dve(lambda e: e.tensor_tensor(kt[:, k, :], ps[:, n % 3, :], dec[:, ds, :], ALU.mult), [('ps', n % 3), ('dec', ds)], [('kt', k)])
            if od % 2 == 1 and i == 0:
                S.dve(lambda e: e.memset(kt[0:1, k, :], 0.0), [], [('kt', k)])

        def st2(n):
            cc, i, od, ds, k = dec_n(n)
            S.act(lambda e: e.copy(kb[:, k, :], kt[:, k, :]), [('kt', k)], [('kb', k)])
            if od % 2 == 0:
                S.dma('sp', g.Gt[od // 2, i * 128:(i + 1) * 128, cc * 512:(cc + 1) * 512], kb[:, k, :], reads=[('kb', k)])
            else:
                S.pe(lambda e: e.matmul(psr[:, 0, :], jr[:], kb[:, k, :], start=True, stop=True), ['jr', ('kb', k)], ['psr'])
            if n % 2 == 0:
                S.pool(lambda e: e.tensor_tensor(sq[:, k, :], kt[:, k, :], kt[:, k, :], ALU.mult), [('kt', k)], [('sq', k)])
            else:
                S.act(lambda e: e.activation(sq[:, k, :], kt[:, k, :], AF.Square), [('kt', k)], [('sq', k)])

        def st3(n):
            cc, i, od, ds, k = dec_n(n)
            S.pe(lambda e: e.matmul(pss[0:1, od, :], ones[:, 0:1], sq[:, k, :], start=(i == 0), stop=(i == 31)), [('sq', k), 'ones'], [('pss', od)])
            if od % 2 == 1:
                r2 = (n // 2) % 2
                S.dve(lambda e: e.tensor_copy(kr[:, r2, :], psr[:, 0, :]), ['psr'], [('kr', r2)])
                r0 = 8065 - 128 * i
                nrow = 127 if i == 0 else 128
                S.dma('act', g.Gt[od // 2, r0:r0 + nrow, cc * 512:(cc + 1) * 512], kr[0:nrow, r2, :], reads=[('kr', r2)])
            if i == 31:
                col0 = od * 2048 + cc * 512
                S.act(lambda e: e.copy(ssr[0:1, col0:col0 + 512], pss[0:1, od, :]), [('pss', od)], ['ssr'])

        skew(512, [st0, st1, st2, st3])
        for o in range(2):
            S.dve(lambda e, o=o: e.tensor_tensor(nr[0:1, o * 2048:(o + 1) * 2048], ssr[0:1, (2 * o) * 2048:(2 * o + 1) * 2048],
                                                 ssr[0:1, (2 * o + 1) * 2048:(2 * o + 2) * 2048], ALU.add), ['ssr'], ['nr'])
        S.dve(lambda e: e.tensor_scalar_add(nr[:], nr[:], 1e-12), ['nr'], ['nr'])
        S.act(lambda e: e.activation(nr[:], nr[:], AF.Sqrt), ['nr'], ['nr'])
        S.dve(lambda e: e.reciprocal(nr[:], nr[:]), ['nr'], ['nr'])
        S.dma('sp', g.nrm.rearrange("(a o) c -> a (o c)", a=1), nr[:], reads=['nr'])
        S.emit()


def ph_fft_s1(nc, ctx, g, jobs):
    with ExitStack() as es:
        F1 = _alloc(es, nc, "F1", [64, 128, 128], BF16)
        X = _alloc(es, nc, "X", [64, 3, 4, D], BF16)
        A = _alloc(es, nc, "A", [128, 3, 2, D], BF16)
        ps = _palloc(es, nc, "ps", [128, 8, 512], F32)
        S = Sched(nc, ctx)
        S.dma('sp', F1[:], g.F1c, writes=['F1'])
        nx = 0
        na = 0
        nm = 0
        for src, dst in jobs:
            sv = src.rearrange("(n1 n2) c -> n1 n2 c", n2=128)
            KR = src.shape[0] // 128
            for gq in range(32):
                xs = nx % 3
                nx += 1
                S.dma('sp', X[0:KR, xs, :, :], sv[:, gq * 4:(gq + 1) * 4, :], writes=[('X', xs)])
                for jj in range(4):
                    n2 = gq * 4 + jj
                    a = na % 3
                    for cg in range(4):
                        b = nm % 8
                        S.pe(lambda e, b=b, n2=n2, xs=xs, jj=jj, KR=KR, cg=cg: e.matmul(ps[:, b, :], F1[0:KR, n2, :], X[0:KR, xs, jj, cg * 512:(cg + 1) * 512], start=True, stop=True),
                             ['F1', ('X', xs)], [('ps', b)])
                        if nm % 2 == 0:
                            S.act(lambda e, b=b, a=a, jj=jj, cg=cg: e.copy(A[:, a, jj % 2, cg * 512:(cg + 1) * 512], ps[:, b, :]), [('ps', b)], [('A', a, jj % 2, cg)])
                        else:
                            S.dve(lambda e, b=b, a=a, jj=jj, cg=cg: e.tensor_copy(A[:, a, jj % 2, cg * 512:(cg + 1) * 512], ps[:, b, :]), [('ps', b)], [('A', a, jj % 2, cg)])
                        nm += 1
                    if jj % 2 == 1:
                        n20 = n2 - 1
                        S.dma('act', dst[:, n20:n20 + 2, :], A[:, a, :, :], reads=[('A', a, q, c_) for q in range(2) for c_ in range(4)])
                        na += 1
        S.emit()


def ph_filter_s2(nc, ctx, g, o):
    with ExitStack() as es:
        Cm = _alloc(es, nc, "Cm", [128, 4, 64], BF16)
        Ain = _alloc(es, nc, "Ain", [128, 3, 4, 512], BF16)
        nb = _alloc(es, nc, "nb", [128, D], F32)
        ko = _alloc(es, nc, "ko", [128, 3, 2, 512], F32)
        ps = _palloc(es, nc, "ps", [128, 4, 2, 512], F32)
        S = Sched(nc, ctx)
        S.dma('sp', Cm[:], g.Cmat, writes=['Cm'])
        S.dma('sp', nb[:], g.nrm[o].partition_broadcast(128), writes=['nb'])

        def dn(n):
            cg, pair = divmod(n, 32)
            return cg * 512, pair

        def st_load(n):
            c0, pair = dn(n)
            s = n % 3
            for half in range(2):
                k1 = 2 * pair + half
                S.dma('sp', Ain[:, s, half * 2 + 0, :], g.Ascr[k1, :, c0:c0 + 512], writes=[('Ain', s, half * 2 + 0)])
                S.dma('act', Ain[:, s, half * 2 + 1, :], g.Ascr[64 + k1, :, c0:c0 + 512], writes=[('Ain', s, half * 2 + 1)])

        def st_pe(n):
            s = n % 3
            pb = n % 4
            for half in range(2):
                rows = slice(half * 64, (half + 1) * 64)
                for ri, seq in enumerate([[(0, 0), (2, 1)], [(1, 0), (0, 1)]]):
                    for t, (mi, ai) in enumerate(seq):
                        S.pe(lambda e, rows=rows, mi=mi, ai=ai, t=t, ri=ri, half=half: e.matmul(ps[rows, pb, ri, :], Cm[:, mi, :], Ain[:, s, half * 2 + ai, :], start=(t == 0), stop=(t == 1)),
                             ['Cm', ('Ain', s, half * 2 + ai)], [('ps', pb)])

        def st_ev(n):
            c0, pair = dn(n)
            pb = n % 4
            kk = n % 3
            S.dve(lambda e: e.tensor_tensor(ko[:, kk, 0, :], ps[:, pb, 0, :], nb[:, c0:c0 + 512], ALU.mult), [('ps', pb), 'nb'], [('ko', kk, 0)])
            S.dve(lambda e: e.tensor_tensor(ko[:, kk, 1, :], ps[:, pb, 1, :], nb[:, c0:c0 + 512], ALU.mult), [('ps', pb), 'nb'], [('ko', kk, 1)])
            S.dma('act', g.KF[o, pair, :, :, c0:c0 + 512].rearrange("r p c -> p r c"), ko[:, kk, :, :], reads=[('ko', kk, 0), ('ko', kk, 1)])

        skew(128, [st_load, st_pe, st_ev])
        S.emit()


def ph_conv_mid(nc, ctx, g, o):
    with ExitStack() as es:
        Cm = _alloc(es, nc, "Cm", [128, 4, 64], BF16)
        Em = _alloc(es, nc, "Em", [128, 3, 128], BF16)
        Ain = _alloc(es, nc, "Ain", [128, 2, 4, D], BF16)
        Kin = _alloc(es, nc, "Kin", [128, 2, 2, D], F32)
        tt = _alloc(es, nc, "tt", [128, 3, 4, 512], F32)
        Y = _alloc(es, nc, "Y", [128, 3, 2, 512], BF16)
        Bo = _alloc(es, nc, "Bo", [128, 2, 2, 2, D], BF16)
        pz = _palloc(es, nc, "pz", [128, 2, 2, 512], F32)
        pb_ = _palloc(es, nc, "pbb", [128, 2, 2, 512], F32)
        S = Sched(nc, ctx)
        S.dma('sp', Cm[:], g.Cmat, writes=['Cm'])
        S.dma('sp', Em[:], g.Emat, writes=['Em'])
        Bv = g.Bscr.rearrange("n (r k) c -> n r k c", r=2)

        def st_load(n):
            pair, cg = divmod(n, 4)
            if cg != 0:
                return
            s = pair % 2
            S.dma('act', Kin[:, s, :, :], g.KF[o, pair].rearrange("r p c -> p r c"), writes=[('Kin', s)])
            for half in range(2):
                k1 = 2 * pair + half
                S.dma('sp', Ain[:, s, 2 * half, :], g.Ascr[k1], writes=[('Ain', s, 2 * half)])
                S.dma('sp' if half == 0 else 'act', Ain[:, s, 2 * half + 1, :], g.Ascr[64 + k1], writes=[('Ain', s, 2 * half + 1)])

        def st_z(n):
            pair, cg = divmod(n, 4)
            s = pair % 2
            cs = slice(cg * 512, (cg + 1) * 512)
            zs = n % 2
            for half in range(2):
                rows = slice(half * 64, (half + 1) * 64)
                for ri, seq in enumerate([[(0, 0), (2, 1)], [(1, 0), (0, 1)]]):
                    for t, (mi, ai) in enumerate(seq):
                        S.pe(lambda e, rows=rows, mi=mi, ai=ai, t=t, ri=ri, half=half: e.matmul(pz[rows, zs, ri, :], Cm[:, mi, :], Ain[:, s, 2 * half + ai, cs], start=(t == 0), stop=(t == 1)),
                             ['Cm', ('Ain', s, 2 * half + ai)], [('pz', zs)])

        def st_m(n):
            pair, cg = divmod(n, 4)
            s = pair % 2
            cs = slice(cg * 512, (cg + 1) * 512)
            zs = n % 2
            q = n % 3
            for t, (zi, ki) in enumerate([(0, 0), (1, 1), (0, 1), (1, 0)]):
                S.dve(lambda e, t=t, zi=zi, ki=ki: e.tensor_tensor(tt[:, q, t, :], pz[:, zs, zi, :], Kin[:, s, ki, cs], ALU.mult),
                      [('pz', zs), ('Kin', s)], [('tt', q, t)])
            S.pool(lambda e: e.tensor_tensor(Y[:, q, 0, :], tt[:, q, 0, :], tt[:, q, 1, :], ALU.subtract), [('tt', q, 0), ('tt', q, 1)], [('Y', q, 0)])
            S.pool(lambda e: e.tensor_tensor(Y[:, q, 1, :], tt[:, q, 2, :], tt[:, q, 3, :], ALU.add), [('tt', q, 2), ('tt', q, 3)], [('Y', q, 1)])

        def st_b(n):
            pair, cg = divmod(n, 4)
            s = pair % 2
            cs = slice(cg * 512, (cg + 1) * 512)
            q = n % 3
            for half in range(2):
                rows = slice(half * 64, (half + 1) * 64)
                for ri, seq in enumerate([[(0, 0), (2, 1)], [(1, 0), (0, 1)]]):
                    for t, (mi, yi) in enumerate(seq):
                        S.pe(lambda e, half=half, ri=ri, rows=rows, mi=mi, yi=yi, t=t: e.matmul(pb_[:, half, ri, :], Em[rows, mi, :], Y[rows, q, yi, :], start=(t == 0), stop=(t == 1)),
                             ['Em', ('Y', q, yi)], [('pb', half, ri)])
                    S.act(lambda e, half=half, ri=ri: e.copy(Bo[:, s, ri, half, cs], pb_[:, half, ri, :]), [('pb', half, ri)], [('Bo', s, ri, half, cg)])

        def st_o(n):
            pair, cg = divmod(n, 4)
            if cg != 3:
                return
            s = pair % 2
            S.dma('act', Bv[:, :, 2 * pair:2 * pair + 2, :], Bo[:, s, :, :, :],
                  reads=[('Bo', s, r_, h_, c_) for r_ in range(2) for h_ in range(2) for c_ in range(4)])

        skew(128, [st_load, st_z, st_m, st_b, st_o])
        S.emit()


def ph_conv_is2(nc, ctx, g, o):
    with ExitStack() as es:
        Tm = _alloc(es, nc, "Tm", [128, 128, 32], BF16)
        hb = _alloc(es, nc, "hb", [128, D], F32)
        Bin = _alloc(es, nc, "Bin", [128, 3, 2, D], BF16)
        zin = _alloc(es, nc, "zin", [128, 3, D], F32)
        gin = _alloc(es, nc, "gin", [128, 3, D], F32)
        sg = _alloc(es, nc, "sg", [128, 3 if o == 1 else 1, D], F32)
        t1 = _alloc(es, nc, "t1", [128, 3, D], F32)
        ob = _alloc(es, nc, "ob", [128, 3, D], BF16)
        ps = _palloc(es, nc, "ps", [128, 8, 512], F32)
        S = Sched(nc, ctx)
        S.dma('sp', Tm[:], g.Tmat, writes=['Tm'])
        S.dma('sp', hb[:], g.h_bias[o].partition_broadcast(128), writes=['hb'])
        for nm, tl in [('zin', zin), ('gin', gin), ('sg', sg)]:
            S.dve(lambda e, tl=tl: e.memset(tl[:], 0.0), [], [(nm, s_, h_) for s_ in range(3) for h_ in range(2)])
        zsrc = (g.U0 if o == 0 else g.Z1).rearrange("(n1 n2) c -> n1 n2 c", n2=128)
        gsrc = (g.U1 if o == 0 else g.U2).rearrange("(n1 n2) c -> n1 n2 c", n2=128)
        ssrc = g.Ptm[1:4097, :].rearrange("(n1 n2) c -> n1 n2 c", n2=128)
        z1d = g.Z1.rearrange("(n1 n2) c -> n1 n2 c", n2=128)
        ztd = g.zT.rearrange("(n1 n2) c -> n1 n2 c", n2=128)
        ytd = g.Ytm.rearrange("(n1 n2) c -> n1 n2 c", n2=128)

        def st_load(pr):
            s = pr % 3
            for hf in range(2):
                n2 = 2 * pr + hf
                rows = slice(hf * 64, hf * 64 + 32)
                S.dma('sp', Bin[:, s, hf, :], g.Bscr[n2, :, :], writes=[('Bin', s, hf)])
                S.dma('sp', zin[rows, s, :], zsrc[:, n2, :], writes=[('zin', s, hf)])
                S.dma('act', gin[rows, s, :], gsrc[:, n2, :], writes=[('gin', s, hf)])
                if o == 1:
                    S.dma('act', sg[rows, s, :], ssrc[:, n2, 6144:8192], writes=[('sg', s, hf)])

        def st_pe(pr):
            s = pr % 3
            for cg in range(4):
                b = (pr * 4 + cg) % 8
                for hf in range(2):
                    n2 = 2 * pr + hf
                    rows = slice(hf * 64, hf * 64 + 32)
                    S.pe(lambda e, b=b, rows=rows, n2=n2, s=s, hf=hf, cg=cg: e.matmul(ps[rows, b, :], Tm[:, n2, :], Bin[:, s, hf, cg * 512:(cg + 1) * 512], start=True, stop=True),
                         ['Tm', ('Bin', s, hf)], [('ps', b)])

        def st_e1(pr):
            s = pr % 3
            for cg in range(4):
                b = (pr * 4 + cg) % 8
                cs = slice(cg * 512, (cg + 1) * 512)
                S.pool(lambda e, s=s, cs=cs: e.tensor_tensor(t1[:, s, cs], zin[:, s, cs], hb[:, cs], ALU.mult), [('zin', s, 0), ('zin', s, 1), 'hb'], [('t1', s, cg)])
                S.dve(lambda e, s=s, b=b, cs=cs: e.tensor_tensor(t1[:, s, cs], t1[:, s, cs], ps[:, b, :], ALU.add), [('t1', s, cg), ('ps', b)], [('t1', s, cg)])
                S.dve(lambda e, s=s, cs=cs: e.tensor_tensor(t1[:, s, cs], t1[:, s, cs], gin[:, s, cs], ALU.mult), [('t1', s, cg), ('gin', s, 0), ('gin', s, 1)], [('t1', s, cg)])

        def st_e2(pr):
            s = pr % 3
            t1k = [('t1', s, cg) for cg in range(4)]
            if o == 0:
                S.act(lambda e, s=s: e.copy(ob[:, s, :], t1[:, s, :]), t1k, [('ob', s, c_) for c_ in range(4)])
                for hf in range(2):
                    n2 = 2 * pr + hf
                    rows = slice(hf * 64, hf * 64 + 32)
                    S.dma('act', z1d[:, n2, :], t1[rows, s, :], reads=t1k)
                    S.dma('sp', ztd[:, n2, :], ob[rows, s, :], reads=[('ob', s, c_) for c_ in range(4)])
            else:
                for cg in range(4):
                    cs = slice(cg * 512, (cg + 1) * 512)
                    S.pool(lambda e, s=s, cs=cs: e.tensor_tensor(ob[:, s, cs], t1[:, s, cs], sg[:, s, cs], ALU.mult), [('t1', s, cg), ('sg', s, 0), ('sg', s, 1)], [('ob', s, cg)])
                for hf in range(2):
                    n2 = 2 * pr + hf
                    rows = slice(hf * 64, hf * 64 + 32)
                    S.dma('sp', ytd[:, n2, :], ob[rows, s, :], reads=[('ob', s, c_) for c_ in range(4)])

        skew(64, [st_load, st_pe, st_e1, st_e2])
        S.emit()


def ph_inproj(nc, ctx, g, xsrc, wb, ncol, mode):
    with ExitStack() as es:
        xT = _alloc(es, nc, "xT", [128, 16, L], BF16)
        idf = _alloc(es, nc, "idf", [128, 128], F32)
        xs = _alloc(es, nc, "xs", [128, 2, D], F32)
        wc = _alloc(es, nc, "wc", [128, 2, 16, 512], BF16)
        bb = _alloc(es, nc, "bb", [128, 2, 512], F32)
        ot = _alloc(es, nc, "ot", [128, 3, 512], F32)
        ob = _alloc(es, nc, "ob", [128, 3, 512], BF16)
        ps = _palloc(es, nc, "ps", [128, 8, 512], F32)
        S = Sched(nc, ctx)
        S.dma('sp', idf[:], g.ident_f, writes=['idf'])
        nb = 0
        for i in range(32):
            s = i % 2
            S.dma('sp', xs[:, s, :], xsrc[i * 128:(i + 1) * 128, :], writes=[('xs', s)])
            for q in range(4):
                b = nb % 8
                nb += 1
                for j in range(4):
                    kc = q * 4 + j
                    S.pe(lambda e, b=b, j=j, s=s, kc=kc: e.transpose(ps[:, b, j * 128:(j + 1) * 128], xs[:, s, kc * 128:(kc + 1) * 128], idf[:]),
                         [('xs', s), 'idf'], [('ps', b)])
                dst = xT[:, q * 4:(q + 1) * 4, i * 128:(i + 1) * 128]
                src = ps[:, b, :].rearrange("p (j t) -> p j t", j=4)
                if q % 2 == 0:
                    S.act(lambda e, dst=dst, src=src: e.copy(dst, src), [('ps', b)], [('xT', i, q)])
                else:
                    S.dve(lambda e, dst=dst, src=src: e.tensor_copy(dst, src), [('ps', b)], [('xT', i, q)])
        allx = [('xT', i, q) for i in range(32) for q in range(4)]
        wv = wb.rearrange("(kc p) c -> p kc c", p=128)
        no = 0
        for j in range(ncol // 512):
            ws = j % 2
            S.dma('sp', wc[:, ws, :, :], wv[:, :, j * 512:(j + 1) * 512], writes=[('wc', ws)])
            if mode == 'hy':
                S.dma('sp', bb[:, ws, :], g.b_hin[j * 512:(j + 1) * 512].partition_broadcast(128), writes=[('bb', ws)])
            feat_major = (mode == 'at' and j < 5)
            if not feat_major:
                for i in range(32):
                    b = nb % 8
                    nb += 1
                    o = no % 3
                    no += 1
                    for kc in range(16):
                        S.pe(lambda e, b=b, kc=kc, i=i, ws=ws: e.matmul(ps[:, b, :], xT[:, kc, i * 128:(i + 1) * 128], wc[:, ws, kc, :], start=(kc == 0), stop=(kc == 15)),
                             allx[i * 4:(i + 1) * 4] + [('wc', ws)], [('ps', b)])
                    rows = slice(i * 128, (i + 1) * 128)
                    if mode == 'hy':
                        S.dve(lambda e, b=b, o=o, ws=ws: e.tensor_tensor(ot[:, o, :], ps[:, b, :], bb[:, ws, :], ALU.add), [('ps', b), ('bb', ws)], [('ot', o)])
                        if j >= 12:
                            S.act(lambda e, o=o: e.activation(ot[:, o, :], ot[:, o, :], AF.Silu), [('ot', o)], [('ot', o)])
                        S.dma('act', g.Ptm[1 + i * 128:1 + (i + 1) * 128, j * 512:(j + 1) * 512], ot[:, o, :], reads=[('ot', o)])
                    elif j == 5:
                        S.act(lambda e, b=b, o=o: e.copy(ob[:, o, :], ps[:, b, :]), [('ps', b)], [('ob', o)])
                        S.dma('act', g.Vs[:, rows, :].rearrange("k t d -> t k d"), ob[:, o, :].rearrange("p (k d) -> p k d", k=4), reads=[('ob', o)])
                    else:
                        S.act(lambda e, b=b, o=o: e.activation(ot[:, o, :], ps[:, b, :], AF.Silu), [('ps', b)], [('ot', o)])
                        S.dma('act', g.Gtm[rows, (j - 6) * 512:(j - 5) * 512], ot[:, o, :], reads=[('ot', o)])
            else:
                for f in range(4):
                    hh = j * 4 + f
                    for tcn in range(8):
                        b = nb % 8
                        nb += 1
                        o = no % 3
                        no += 1
                        for kc in range(16):
                            S.pe(lambda e, b=b, kc=kc, ws=ws, f=f, tcn=tcn: e.matmul(ps[:, b, :], wc[:, ws, kc, f * 128:(f + 1) * 128], xT[:, kc, tcn * 512:(tcn + 1) * 512], start=(kc == 0), stop=(kc == 15)),
                                 allx[tcn * 16:(tcn + 1) * 16] + [('wc', ws)], [('ps', b)])
                        if no % 2 == 0:
                            S.act(lambda e, b=b, o=o: e.copy(ob[:, o, :], ps[:, b, :]), [('ps', b)], [('ob', o)])
                        else:
                            S.dve(lambda e, b=b, o=o: e.tensor_copy(ob[:, o, :], ps[:, b, :]), [('ps', b)], [('ob', o)])
                        dstq = g.QT[hh, :, tcn * 512:(tcn + 1) * 512] if hh < 16 else g.KT[hh - 16, :, tcn * 512:(tcn + 1) * 512]
                        S.dma('act', dstq, ob[:, o, :], reads=[('ob', o)])
        S.emit()


def ph_inproj_hy2(nc, ctx, g, xsrc):
    with ExitStack() as es:
        xT = _alloc(es, nc, "xT", [128, 16, L + 2], BF16)
        idf = _alloc(es, nc, "idf", [128, 128], F32)
        xs = _alloc(es, nc, "xs", [128, 2, D], F32)
        wc = _alloc(es, nc, "wc", [128, 2, 16, 512], BF16)
        hv = _alloc(es, nc, "hv", [128, 9, 48], F32)
        ub = _alloc(es, nc, "ub", [128, 8, 256], F32)
        ut = _alloc(es, nc, "ut", [128, 4, 512], F32)
        utb = _alloc(es, nc, "utb", [128, 2, 512], BF16)
        bb = _alloc(es, nc, "bb", [128, 2, 512], F32)
        ot = _alloc(es, nc, "ot", [128, 2, 512], F32)
        ps = _palloc(es, nc, "ps", [128, 8, 512], F32)
        S = Sched(nc, ctx)
        S.dma('sp', idf[:], g.ident_f, writes=['idf'])
        S.dma('act', hv[:, 0:5, :], g.hvec, writes=['hv'])
        S.dve(lambda e: e.tensor_tensor(hv[:, 5, :], hv[:, 0, :], hv[:, 1, :], ALU.add), ['hv'], ['hv'])
        S.dve(lambda e: e.tensor_tensor(hv[:, 5, :], hv[:, 5, :], hv[:, 2, :], ALU.add), ['hv'], ['hv'])
        S.dve(lambda e: e.tensor_tensor(hv[:, 6, :], hv[:, 5, :], hv[:, 4, :], ALU.mult), ['hv'], ['hv'])
        S.dve(lambda e: e.tensor_tensor(hv[:, 6, :], hv[:, 6, :], hv[:, 3, :], ALU.add), ['hv'], ['hv'])
        S.dve(lambda e: e.tensor_tensor(hv[:, 7, :], hv[:, 0, :], hv[:, 4, :], ALU.mult), ['hv'], ['hv'])
        S.dve(lambda e: e.tensor_tensor(hv[:, 8, :], hv[:, 2, :], hv[:, 4, :], ALU.mult), ['hv'], ['hv'])
        S.dve(lambda e: e.memset(xT[:, :, 0:1], 0.0), [], ['xTpad0'])
        S.dve(lambda e: e.memset(xT[:, :, L + 1:L + 2], 0.0), [], ['xTpad1'])
        nb = 0
        for i in range(32):
            s = i % 2
            S.dma('sp', xs[:, s, :], xsrc[i * 128:(i + 1) * 128, :], writes=[('xs', s)])
            for q in range(4):
                b = nb % 8
                nb += 1
                for j in range(4):
                    kc = q * 4 + j
                    S.pe(lambda e, b=b, j=j, s=s, kc=kc: e.transpose(ps[:, b, j * 128:(j + 1) * 128], xs[:, s, kc * 128:(kc + 1) * 128], idf[:]),
                         [('xs', s), 'idf'], [('ps', b)])
                dst = xT[:, q * 4:(q + 1) * 4, 1 + i * 128:1 + (i + 1) * 128]
                src = ps[:, b, :].rearrange("p (j t) -> p j t", j=4)
                if q % 2 == 0:
                    S.act(lambda e, dst=dst, src=src: e.copy(dst, src), [('ps', b)], [('xT', i, q)])
                else:
                    S.dve(lambda e, dst=dst, src=src: e.tensor_copy(dst, src), [('ps', b)], [('xT', i, q)])
        allx = [('xT', i, q) for i in range(32) for q in range(4)] + ['xTpad0', 'xTpad1']
        wv = g.wb_hin.rearrange("(kc p) c -> p kc c", p=128)
        outs = [g.U0, g.U1, g.U2]

        def dn(n):
            j, tc = divmod(n, 16)
            return j, tc, j // 4, (j % 4) * 512, j % 2

        def st_w(n):
            j, tc, q, c0, ws = dn(n)
            if tc == 0:
                S.dma('sp', wc[:, ws, :, :], wv[:, :, j * 512:(j + 1) * 512], writes=[('wc', ws)])

        def st_mm(n):
            j, tc, q, c0, ws = dn(n)
            t0 = tc * 256
            for f in range(4):
                for kc in range(16):
                    S.pe(lambda e, f=f, kc=kc: e.matmul(ps[:, f, 0:258], wc[:, ws, kc, f * 128:(f + 1) * 128], xT[:, kc, t0:t0 + 258], start=(kc == 0), stop=(kc == 15)),
                         allx + [('wc', ws)], [('psm', f)])

        def st_conv(n):
            j, tc, q, c0, ws = dn(n)
            for f in range(4):
                fb = j * 4 + f
                u = ub[:, (n % 2) * 4 + f, :]
                uk = ('ub', (n % 2) * 4 + f)
                S.act(lambda e, f=f, fb=fb, u=u: e.activation(u, ps[:, f, 0:256], AF.Identity, bias=hv[:, 6, fb:fb + 1], scale=hv[:, 0, fb:fb + 1]), [('psm', f), 'hv'], [uk])
                S.dve(lambda e, f=f, fb=fb, u=u: e.scalar_tensor_tensor(u, ps[:, f, 1:257], hv[:, 1, fb:fb + 1], u, ALU.mult, ALU.add), [('psm', f), 'hv', uk], [uk])
                S.dve(lambda e, f=f, fb=fb, u=u: e.scalar_tensor_tensor(u, ps[:, f, 2:258], hv[:, 2, fb:fb + 1], u, ALU.mult, ALU.add), [('psm', f), 'hv', uk], [uk])
                if tc == 0:
                    S.dve(lambda e, fb=fb, u=u: e.tensor_tensor(u[:, 0:1], u[:, 0:1], hv[:, 7, fb:fb + 1], ALU.subtract), [uk, 'hv'], [uk])
                if tc == 15:
                    S.dve(lambda e, fb=fb, u=u: e.tensor_tensor(u[:, 255:256], u[:, 255:256], hv[:, 8, fb:fb + 1], ALU.subtract), [uk, 'hv'], [uk])

        def st_tr(n):
            for sub in range(2):
                for f in range(4):
                    S.pe(lambda e, sub=sub, f=f: e.transpose(ps[:, 4 + sub, f * 128:(f + 1) * 128], ub[:, (n % 2) * 4 + f, sub * 128:(sub + 1) * 128], idf[:]),
                         [('ub', (n % 2) * 4 + f), 'idf'], [('pst', sub)])

        def st_ev(n):
            j, tc, q, c0, ws = dn(n)
            for sub in range(2):
                us = (n % 2) * 2 + sub
                r0 = tc * 256 + sub * 128
                if sub == 0:
                    S.act(lambda e, us=us, sub=sub: e.copy(ut[:, us, :], ps[:, 4 + sub, :]), [('pst', sub)], [('ut', us)])
                else:
                    S.dve(lambda e, us=us, sub=sub: e.tensor_copy(ut[:, us, :], ps[:, 4 + sub, :]), [('pst', sub)], [('ut', us)])
                S.dma('act', outs[q][r0:r0 + 128, c0:c0 + 512], ut[:, us, :], reads=[('ut', us)])
                if q == 0:
                    S.pool(lambda e, us=us, sub=sub: e.tensor_copy(utb[:, sub, :], ut[:, us, :]), [('ut', us)], [('utb', sub)])
                    S.dma('sp', g.zT[r0:r0 + 128, c0:c0 + 512], utb[:, sub, :], reads=[('utb', sub)])

        skew(192, [st_w, st_mm, st_conv, st_tr, st_ev])

        no = 0
        for j in range(12, 16):
            ws = j % 2
            S.dma('sp', wc[:, ws, :, :], wv[:, :, j * 512:(j + 1) * 512], writes=[('wc', ws)])
            S.dma('sp', bb[:, ws, :], g.b_hin[j * 512:(j + 1) * 512].partition_broadcast(128), writes=[('bb', ws)])
            for i in range(32):
                b = 6 + (no % 2)
                o = no % 2
                no += 1
                for kc in range(16):
                    S.pe(lambda e, b=b, kc=kc, i=i, ws=ws: e.matmul(ps[:, b, :], xT[:, kc, 1 + i * 128:1 + (i + 1) * 128], wc[:, ws, kc, :], start=(kc == 0), stop=(kc == 15)),
                         allx + [('wc', ws)], [('psg', b)])
                S.dve(lambda e, b=b, o=o, ws=ws: e.tensor_tensor(ot[:, o, :], ps[:, b, :], bb[:, ws, :], ALU.add), [('psg', b), ('bb', ws)], [('ot', o)])
                S.act(lambda e, o=o: e.activation(ot[:, o, :], ot[:, o, :], AF.Silu), [('ot', o)], [('ot', o)])
                S.dma('act', g.Ptm[1 + i * 128:1 + (i + 1) * 128, j * 512:(j + 1) * 512], ot[:, o, :], reads=[('ot', o)])
        S.emit()


def ph_sconv(nc, ctx, g):
    with ExitStack() as es:
        idf = _alloc(es, nc, "idf", [128, 128], F32)
        wbc = _alloc(es, nc, "wbc", [128, 2, 12, 512], F32)
        pin = _alloc(es, nc, "pin", [128, 2, 9, 512], F32)
        prd = _alloc(es, nc, "prd", [128, 2, 9, 512], F32)
        ou = _alloc(es, nc, "ou", [128, 2, 3, 512], F32)
        zb = _alloc(es, nc, "zb", [128, 2, 512], BF16)
        ps = _palloc(es, nc, "ps", [128, 2, 3, 512], F32)
        S = Sched(nc, ctx)
        S.dma('sp', idf[:], g.ident_f, writes=['idf'])
        outs = [g.U0, g.U1, g.U2]

        def st_load(n):
            cg, i = divmod(n, 32)
            c0 = cg * 512
            s = n % 2
            if i == 0:
                ws = cg % 2
                for q in range(3):
                    for j in range(3):
                        S.dma('act', wbc[:, ws, q * 4 + j, :], g.w_sc[j, q * 2048 + c0:q * 2048 + c0 + 512].partition_broadcast(128), writes=[('wbc', ws, q * 4 + j)])
                    S.dma('act', wbc[:, ws, q * 4 + 3, :], g.b_sc[q * 2048 + c0:q * 2048 + c0 + 512].partition_broadcast(128), writes=[('wbc', ws, q * 4 + 3)])
            for q in range(3):
                for j in range(3):
                    S.dma('sp', pin[:, s, q * 3 + j, :], g.Ptm[i * 128 + j:i * 128 + j + 128, q * 2048 + c0:q * 2048 + c0 + 512], writes=[('pin', s, q * 3 + j)])

        def st_mul(n):
            cg, i = divmod(n, 32)
            s = n % 2
            ws = cg % 2
            for m in range(9):
                q, j = divmod(m, 3)
                eng = 'dve' if m in (0, 2, 4, 6) else 'pool'
                S.add(eng, lambda e, m=m, q=q, j=j: e.tensor_tensor(prd[:, s, m, :], pin[:, s, m, :], wbc[:, ws, q * 4 + j, :], ALU.mult),
                      [('pin', s, m), ('wbc', ws, q * 4 + j)], [('prd', s, m)])

        def st_pe(n):
            s = n % 2
            for q in range(3):
                for j in range(3):
                    S.pe(lambda e, q=q, j=j: e.matmul(ps[:, s, q, :], idf[:], prd[:, s, q * 3 + j, :], start=(j == 0), stop=(j == 2)),
                         ['idf', ('prd', s, q * 3 + j)], [('ps', s, q)])

        def st_out(n):
            cg, i = divmod(n, 32)
            c0 = cg * 512
            s = n % 2
            ws = cg % 2
            for q in range(3):
                S.dve(lambda e, q=q: e.tensor_tensor(ou[:, s, q, :], ps[:, s, q, :], wbc[:, ws, q * 4 + 3, :], ALU.add), [('ps', s, q), ('wbc', ws, q * 4 + 3)], [('ou', s, q)])
                S.dma('act', outs[q][i * 128:(i + 1) * 128, c0:c0 + 512], ou[:, s, q, :], reads=[('ou', s, q)])
                if q == 0:
                    S.act(lambda e: e.copy(zb[:, s, :], ou[:, s, 0, :]), [('ou', s, 0)], [('zb', s)])
                    S.dma('act', g.zT[i * 128:(i + 1) * 128, c0:c0 + 512], zb[:, s, :], reads=[('zb', s)])

        skew(128, [st_load, st_mul, st_pe, st_out])
        S.emit()


def ph_outproj(nc, ctx, g, ysrc, xsrc, wb, bias, lg, lb, dst):
    with ExitStack() as es:
        W = _alloc(es, nc, "W", [128, 16, D], BF16)
        idb = _alloc(es, nc, "idb", [128, 128], BF16)
        yt = _alloc(es, nc, "yt", [128, 2, D], BF16)
        yT = _alloc(es, nc, "yT", [128, 2, 16, 128], BF16)
        xr = _alloc(es, nc, "xr", [128, 3, D], F32)
        sm = _alloc(es, nc, "sm", [128, 4, D], F32)
        oo = _alloc(es, nc, "oo", [128, 2, D], F32)
        jk = _alloc(es, nc, "jk", [128, D], BF16)
        gb = _alloc(es, nc, "gb", [128, 3, D], F32)
        st = _alloc(es, nc, "st", [128, 4, 8], F32)
        pt = _palloc(es, nc, "pt", [128, 16, 128], BF16)
        ps = _palloc(es, nc, "ps", [128, 4, 512], F32)
        S = Sched(nc, ctx)
        S.dma('sp', W[:], wb.rearrange("(kc p) c -> p kc c", p=128), writes=['W'])
        S.dma('sp', idb[:], g.ident_b, writes=['idb'])
        S.dma('sp', gb[:, 0, :], lg.partition_broadcast(128), writes=['gb'])
        S.dma('sp', gb[:, 1, :], lb.partition_broadcast(128), writes=['gb'])
        if bias is not None:
            S.dma('sp', gb[:, 2, :], bias.partition_broadcast(128), writes=['gb'])

        def rows(i):
            return slice(i * 128, (i + 1) * 128)

        def st_l(i):
            S.dma('sp', yt[:, i % 2, :], ysrc[rows(i), :], writes=[('yt', i % 2)])

        def st_t(i):
            s = i % 2
            for kc in range(16):
                S.pe(lambda e, kc=kc: e.transpose(pt[:, kc, :], yt[:, s, kc * 128:(kc + 1) * 128], idb[:]), [('yt', s), 'idb'], ['pt'])

        def st_e(i):
            s = i % 2
            S.dma('act', xr[:, i % 3, :], xsrc[rows(i), :], writes=[('xr', i % 3)])
            S.act(lambda e: e.copy(yT[:, s, 0:8, :], pt[:, 0:8, :]), ['pt'], [('yT', s, 0)])
            S.dve(lambda e: e.tensor_copy(yT[:, s, 8:16, :], pt[:, 8:16, :]), ['pt'], [('yT', s, 1)])

        def st_m(i):
            s = i % 2
            for c in range(4):
                for kc in range(16):
                    S.pe(lambda e, c=c, kc=kc: e.matmul(ps[:, c, :], yT[:, s, kc, :], W[:, kc, c * 512:(c + 1) * 512], start=(kc == 0), stop=(kc == 15)),
                         [('yT', s, 0), ('yT', s, 1), 'W'], [('ps', c)])

        def smk(i):
            return [('sm', i % 4, c) for c in range(4)]

        def st_r(i):
            s4, s3 = i % 4, i % 3
            for c in range(4):
                cs = slice(c * 512, (c + 1) * 512)
                if bias is not None:
                    S.dve(lambda e, c=c, cs=cs: e.tensor_tensor(sm[:, s4, cs], ps[:, c, :], gb[:, 2, cs], ALU.add), [('ps', c), 'gb'], [('sm', s4, c)])
                    S.dve(lambda e, cs=cs: e.scalar_tensor_tensor(sm[:, s4, cs], xr[:, s3, cs], ALPHA, sm[:, s4, cs], ALU.mult, ALU.add),
                          [('xr', s3), ('sm', s4, c)], [('sm', s4, c)])
                else:
                    S.dve(lambda e, c=c, cs=cs: e.scalar_tensor_tensor(sm[:, s4, cs], xr[:, s3, cs], ALPHA, ps[:, c, :], ALU.mult, ALU.add),
                          [('xr', s3), ('ps', c)], [('sm', s4, c)])

        def st_a1(i):
            s4 = i % 4
            S.act(lambda e: e.activation(jk[:], sm[:, s4, :], AF.Identity, accum_out=st[:, s4, 0:1]), smk(i), ['jk', ('st', s4, 0)])

        def st_v1(i):
            s4 = i % 4
            S.dve(lambda e: e.tensor_single_scalar(st[:, s4, 1:2], st[:, s4, 0:1], -1.0 / D, ALU.mult), [('st', s4, 0)], [('st', s4, 1)])

        def st_a2(i):
            s4 = i % 4
            S.act(lambda e: e.activation(jk[:], sm[:, s4, :], AF.Square, bias=st[:, s4, 1:2], accum_out=st[:, s4, 2:3]), smk(i) + [('st', s4, 1)], ['jk', ('st', s4, 2)])

        def st_v2(i):
            s4 = i % 4
            S.dve(lambda e: e.tensor_scalar(st[:, s4, 3:4], st[:, s4, 2:3], 1.0 / D, LN_EPS, ALU.mult, ALU.add), [('st', s4, 2)], [('st', s4, 2)])

        def st_a3(i):
            s4 = i % 4
            S.act(lambda e: e.activation(st[:, s4, 4:5], st[:, s4, 3:4], AF.Sqrt), [('st', s4, 2)], [('st', s4, 3)])

        def st_n(i):
            s4, s2 = i % 4, i % 2
            S.dve(lambda e: e.reciprocal(st[:, s4, 5:6], st[:, s4, 4:5]), [('st', s4, 3)], [('st', s4, 3)])
            S.dve(lambda e: e.tensor_scalar(oo[:, s2, :], sm[:, s4, :], st[:, s4, 1:2], st[:, s4, 5:6], ALU.add, ALU.mult),
                  smk(i) + [('st', s4, 1), ('st', s4, 3)], [('oo', s2)])

        def st_g(i):
            s2 = i % 2
            S.pool(lambda e: e.tensor_tensor(oo[:, s2, :], oo[:, s2, :], gb[:, 0, :], ALU.mult), [('oo', s2), 'gb'], [('oo', s2)])
            S.pool(lambda e: e.tensor_tensor(oo[:, s2, :], oo[:, s2, :], gb[:, 1, :], ALU.add), [('oo', s2), 'gb'], [('oo', s2)])
            S.dma('act', dst[rows(i), :], oo[:, s2, :], reads=[('oo', s2)])

        def st_va(i):
            st_v1(i)
            st_a2(i)

        def st_vn(i):
            st_v2(i)
            st_a3(i)
            st_n(i)

        skew(32, [st_l, st_t, st_e, st_m, st_r, st_a1, st_va, st_vn, st_g])
        S.emit()


def ph_attn(nc, ctx, g):
    with ExitStack() as es:
        idb = _alloc(es, nc, "idb", [128, 128], BF16)
        ab = _alloc(es, nc, "ab", [128, 16, 384], F32)
        sk = _alloc(es, nc, "sk", [128, 16], F32)
        kt = _alloc(es, nc, "kt", [128, 2, L], BF16)
        vv = _alloc(es, nc, "vv", [128, 2, 32, 128], BF16)
        qt = _alloc(es, nc, "qt", [128, 2, L], BF16)
        gg = _alloc(es, nc, "gg", [128, 2, 32, 128], F32)
        yh = _alloc(es, nc, "yh", [128, 2, 32, 128], BF16)
        lgt = _alloc(es, nc, "lgt", [128, 3, 384], F32)
        pe_ = _alloc(es, nc, "pe_", [128, 3, 384], F32)
        pn = _alloc(es, nc, "pn", [128, 3, 384], BF16)
        pT = _alloc(es, nc, "pT", [128, 3, 3, 128], BF16)
        sc = _alloc(es, nc, "sc", [128, 6, 8], F32)
        pss = _palloc(es, nc, "pss", [128, 3, 512], F32)
        ptt = _palloc(es, nc, "ptt", [128, 2, 8, 128], BF16)
        po = _palloc(es, nc, "po", [128, 3, 512], F32)
        S = Sched(nc, ctx)
        S.dma('sp', idb[:], g.ident_b, writes=['idb'])
        S.dma('sp', ab[:], g.abias, writes=['ab'])
        S.dma('sp', sk[:], g.sink.partition_broadcast(128), writes=['sk'])
        yv = g.Ytm.rearrange("(i p) c -> p i c", p=128)
        gv = g.Gtm.rearrange("(i p) c -> p i c", p=128)

        def load_head(h):
            kvh = h // 4
            ks = kvh % 2
            hs = h % 2
            if h % 4 == 0:
                S.dma('sp', kt[:, ks, :], g.KT[kvh], writes=[('kt', ks)])
                S.dma('act', vv[:, ks, :, :], g.Vs[kvh].rearrange("(i p) d -> p i d", p=128), writes=[('vv', ks)])
            S.dma('sp', qt[:, hs, :], g.QT[h], writes=[('qt', hs)])
            S.dma('act', gg[:, hs, :, :], gv[:, :, h * 128:(h + 1) * 128], writes=[('gg', hs)])

        def geom(n):
            h, i = divmod(n, 32)
            lo, hi = max(i - 1, 0), min(i + 1, 31)
            return h, i, lo, hi, (lo - i + 1) * 128, (hi - i + 2) * 128

        def st0(n):
            h, i, lo, hi, c0, c1 = geom(n)
            if i == 8 and h + 1 < 16:
                load_head(h + 1)
            s = n % 3
            hs, ks = h % 2, (h // 4) % 2
            for jb in range(lo, hi + 1):
                blk = jb - i + 1
                S.pe(lambda e, s=s, blk=blk, hs=hs, ks=ks, i=i, jb=jb: e.matmul(pss[:, s, blk * 128:(blk + 1) * 128], qt[:, hs, i * 128:(i + 1) * 128], kt[:, ks, jb * 128:(jb + 1) * 128], start=True, stop=True),
                     [('qt', hs), ('kt', ks)], [('pss', s)])

        def st1(n):
            h, i, lo, hi, c0, c1 = geom(n)
            s = n % 3
            q = n % 6
            S.dve(lambda e: e.scalar_tensor_tensor(lgt[:, s, c0:c1], pss[:, s, c0:c1], SCALE, ab[:, h, c0:c1], ALU.mult, ALU.add),
                  [('pss', s), 'ab'], [('lgt', s)])
            S.dve(lambda e: e.reduce_max(sc[:, q, 0:1], lgt[:, s, c0:c1], AX.X), [('lgt', s)], [('sc', q, 0)])
            S.dve(lambda e: e.tensor_scalar(sc[:, q, 2:3], sc[:, q, 0:1], sk[:, h:h + 1], -1.0, ALU.max, ALU.mult), [('sc', q, 0), 'sk'], [('sc', q, 0)])

        def st2(n):
            h, i, lo, hi, c0, c1 = geom(n)
            s = n % 3
            q = n % 6
            S.act(lambda e: e.activation(pn[:, s, c0:c1], lgt[:, s, c0:c1], AF.Exp, bias=sc[:, q, 2:3], accum_out=sc[:, q, 3:4]),
                  [('lgt', s), ('sc', q, 0)], [('pn', s), ('sc', q, 1)])
            S.act(lambda e: e.activation(sc[:, q, 4:5], sk[:, h:h + 1], AF.Exp, bias=sc[:, q, 2:3]), [('sc', q, 0), 'sk'], [('sc', q, 1)])

        def st3(n):
            h, i, lo, hi, c0, c1 = geom(n)
            s = n % 3
            q = n % 6
            S.dve(lambda e: e.tensor_tensor(sc[:, q, 5:6], sc[:, q, 3:4], sc[:, q, 4:5], ALU.add), [('sc', q, 1)], [('sc', q, 2)])
            S.dve(lambda e: e.reciprocal(sc[:, q, 6:7], sc[:, q, 5:6]), [('sc', q, 2)], [('sc', q, 2)])

        def st4(n):
            h, i, lo, hi, c0, c1 = geom(n)
            s = n % 3
            t2 = n % 2
            for jb in range(lo, hi + 1):
                blk = jb - i + 1
                S.pe(lambda e, blk=blk: e.transpose(ptt[:, t2, blk, :], pn[:, s, blk * 128:(blk + 1) * 128], idb[:]), [('pn', s), 'idb'], [('ptt', t2)])

        def st5(n):
            h, i, lo, hi, c0, c1 = geom(n)
            s = n % 3
            t2 = n % 2
            b0, b1 = lo - i + 1, hi - i + 2
            S.act(lambda e: e.copy(pT[:, s, b0:b1, :], ptt[:, t2, b0:b1, :]), [('ptt', t2)], [('pT', s)])

        def st6(n):
            h, i, lo, hi, c0, c1 = geom(n)
            s = n % 3
            ks = (h // 4) % 2
            for jb in range(lo, hi + 1):
                blk = jb - i + 1
                S.pe(lambda e, blk=blk, jb=jb: e.matmul(po[:, s, 0:128], pT[:, s, blk, :], vv[:, ks, jb, :], start=(jb == lo), stop=(jb == hi)),
                     [('pT', s), ('vv', ks)], [('po', s)])

        def st7(n):
            h, i, lo, hi, c0, c1 = geom(n)
            s = n % 3
            hs = h % 2
            q = n % 6
            S.dve(lambda e: e.scalar_tensor_tensor(yh[:, hs, i, :], po[:, s, 0:128], sc[:, q, 6:7], gg[:, hs, i, :], ALU.mult, ALU.mult),
                  [('po', s), ('gg', hs), ('sc', q, 2)], [('yh', hs)])
            if i == 31:
                S.dma('act', yv[:, :, h * 128:(h + 1) * 128], yh[:, hs, :, :], reads=[('yh', hs)])

        load_head(0)
        skew(512, [st0, st1, st2, st3, st4, st5, st6, st7])
        S.emit()


_DBG = []


def build(nslots=2, stop_after=None):
    nc = bass.Bass("TRN2", target_bir_lowering=False)
    g = K()

    def inp(name, shape, dt=F32):
        return nc.dram_tensor(name, list(shape), dt, kind="ExternalInput").ap()

    def scr(name, shape, dt):
        kind = "ExternalOutput" if name in _DBG else "Internal"
        return nc.dram_tensor(name, list(shape), dt, kind=kind).ap()

    g.x_in = inp("x_in", [nslots, L, D])
    g.w_hin = inp("w_hin", [D, 4 * D]); g.b_hin = inp("b_hin", [4 * D])
    g.w_sc = inp("w_sc", [3, 3 * D]); g.b_sc = inp("b_sc", [3 * D])
    g.w_f1 = inp("w_f1", [33, 64]); g.b_f1 = inp("b_f1", [64]); g.fr1 = inp("fr1", [64])
    g.w_f2 = inp("w_f2", [64, 64]); g.b_f2 = inp("b_f2", [64]); g.fr2 = inp("fr2", [64])
    g.w_f3 = inp("w_f3", [64, 64]); g.b_f3 = inp("b_f3", [64]); g.fr3 = inp("fr3", [64])
    g.w_f4 = inp("w_f4", [64, 4 * D]); g.h_bias = inp("h_bias", [2, D])
    g.w_hout = inp("w_hout", [D, D]); g.b_hout = inp("b_hout", [D])
    g.w_ain = inp("w_ain", [D, 5120]); g.sink = inp("sink", [16]); g.w_aout = inp("w_aout", [D, D])
    g.ln_g = inp("ln_g", [2, D]); g.ln_b = inp("ln_b", [2, D])
    g.ident_f = inp("ident_f", [128, 128]); g.ident_b = inp("ident_b", [128, 128], BF16)
    g.featT = inp("featT", [33, L]); g.tnorm = inp("tnorm", [128, 32]); g.negdelta = inp("negdelta", [D])
    g.F1c = inp("F1c", [64, 128, 128], BF16); g.Jrev = inp("Jrev", [128, 128], BF16); g.Cmat = inp("Cmat", [128, 4, 64], BF16)
    g.Emat = inp("Emat", [128, 3, 128], BF16); g.Tmat = inp("Tmat", [128, 128, 32], BF16)
    g.abias = inp("abias", [128, 16, 384]); g.hvec = inp("hvec", [128, 5, 48])
    g.y_out = nc.dram_tensor("y_out", [nslots, L, D], F32, kind="ExternalOutput").ap()

    g.wb_hin = scr("wb_hin", [D, 4 * D], BF16); g.wb_hout = scr("wb_hout", [D, D], BF16)
    g.wb_ain = scr("wb_ain", [D, 5120], BF16); g.wb_aout = scr("wb_aout", [D, D], BF16)
    g.H3 = scr("H3", [64, L], F32)
    g.Gt = scr("Gt", [2, 2 * L, D], BF16)
    g.nrm = scr("nrm", [2, D], F32)
    g.KF = scr("KF", [2, 32, 2, 128, D], F32)
    g.Ascr = scr("Ascr", [128, 128, D], BF16)
    g.Bscr = scr("Bscr", [128, 128, D], BF16)
    g.Ptm = scr("Ptm", [L + 2, 4 * D], F32)
    g.U0 = scr("U0", [L, D], F32); g.U1 = scr("U1", [L, D], F32); g.U2 = scr("U2", [L, D], F32)
    g.Z1 = scr("Z1", [L, D], F32)
    g.zT = scr("zT", [L, D], BF16)
    g.Ytm = scr("Ytm", [L, D], BF16)
    g.X1 = scr("X1", [L, D], F32)
    g.QT = scr("QT", [16, 128, L], BF16); g.KT = scr("KT", [4, 128, L], BF16)
    g.Vs = scr("Vs", [4, L, 128], BF16); g.Gtm = scr("Gtm", [L, D], F32)

    ctx = SemCtx()

    def done(tag):
        return stop_after is not None and stop_after == tag

    ph_cast(nc, ctx, [(g.w_hin, g.wb_hin, D, 4 * D), (g.w_hout, g.wb_hout, D, D),
                      (g.w_ain, g.wb_ain, D, 5120), (g.w_aout, g.wb_aout, D, D)])
    ph_zero_rows(nc, ctx, g)
    ph_filter_mlp(nc, ctx, g)
    ph_filter_gen(nc, ctx, g)
    if done('fgen'):
        return nc
    for o in range(2):
        ph_fft_s1(nc, ctx, g, [(g.Gt[o], g.Ascr)])
        ph_filter_s2(nc, ctx, g, o)
    if done('filter'):
        return nc
    for slot in range(nslots):
        x0 = g.x_in[slot]
        ph_inproj_hy2(nc, ctx, g, x0)
        if done('sconv'):
            return nc
        for o in range(2):
            ph_fft_s1(nc, ctx, g, [(g.zT, g.Ascr)])
            ph_conv_mid(nc, ctx, g, o)
            ph_conv_is2(nc, ctx, g, o)
            if done('conv%d' % o):
                return nc
        ph_outproj(nc, ctx, g, g.Ytm, x0, g.wb_hout, g.b_hout, g.ln_g[0], g.ln_b[0], g.X1)
        if done('hy'):
            return nc
        ph_inproj(nc, ctx, g, g.X1, g.wb_ain, 5120, 'at')
        ph_attn(nc, ctx, g)
        ph_outproj(nc, ctx, g, g.Ytm, g.X1, g.wb_aout, None, g.ln_g[1], g.ln_b[1], g.y_out[slot])
    return nc


def make_in_maps(inputs, nslots=2):
    c = _consts()
    f32 = lambda a: np.ascontiguousarray(np.asarray(a, dtype=np.float32))
    shared = {
        "w_hin": f32(inputs["hy_w_in"][0]), "b_hin": f32(inputs["hy_b_in"][0]),
        "w_sc": f32(inputs["hy_w_sc"][0]), "b_sc": f32(inputs["hy_b_sc"][0]),
        "w_f1": f32(inputs["hy_w_f1"][0]), "b_f1": f32(inputs["hy_b_f1"][0]), "fr1": f32(inputs["hy_fr1"][0]),
        "w_f2": f32(inputs["hy_w_f2"][0]), "b_f2": f32(inputs["hy_b_f2"][0]), "fr2": f32(inputs["hy_fr2"][0]),
        "w_f3": f32(inputs["hy_w_f3"][0]), "b_f3": f32(inputs["hy_b_f3"][0]), "fr3": f32(inputs["hy_fr3"][0]),
        "w_f4": f32(inputs["hy_w_f4"][0]), "h_bias": f32(inputs["hy_h_bias"][0]),
        "w_hout": f32(inputs["hy_w_out"][0]), "b_hout": f32(inputs["hy_b_out"][0]),
        "w_ain": f32(inputs["at_w_in"][0]), "sink": f32(inputs["at_sink"][0]), "w_aout": f32(inputs["at_w_out"][0]),
        "ln_g": f32(inputs["ln_g"]), "ln_b": f32(inputs["ln_b"]),
    }
    shared.update(c)
    hv = np.stack([shared["w_sc"][0], shared["w_sc"][1], shared["w_sc"][2], shared["b_sc"], shared["b_hin"][:3 * D]], axis=0)
    shared["hvec"] = np.ascontiguousarray(hv.reshape(5, 48, 128).transpose(2, 0, 1))
    xs = f32(inputs["x_sample"])
    xp = f32(inputs["x_prompt"])
    maps = []
    for core in range(8):
        second = xp[core] if core < 2 else xs[core]
        m = dict(shared)
        m["x_in"] = np.ascontiguousarray(np.stack([xs[core], second], axis=0)[:nslots])
        maps.append(m)
    return maps


def kernel(**inputs):
    nc = build()
    maps = make_in_maps(inputs)
    res = run_bass_kernel_spmd(nc, maps, core_ids=list(range(8)))
    ys = np.stack([res.results[c]["y_out"][0] for c in range(8)], axis=0).astype(np.float32)
    yp = np.stack([res.results[c]["y_out"][1] for c in range(2)], axis=0).astype(np.float32)
    return (yp, ys)
```

```python
import numpy as np
import concourse.bass as bass
import concourse.mybir as mybir
from concourse.bass_utils import run_bass_kernel_spmd

F32 = mybir.dt.float32
BF16 = mybir.dt.bfloat16
ALU = mybir.AluOpType
AF = mybir.ActivationFunctionType
AX = mybir.AxisListType

SEM_EPOCH = 24000


class Sched:
    def __init__(self, nc, ctx, dma_sems=12):
        self.nc = nc
        self.ctx = ctx
        self.ops = []
        self.last_w = {}
        self.readers = {}
        self.dma_sems = dma_sems

    def add(self, eng, fn, reads=(), writes=(), dma=False):
        i = len(self.ops)
        deps = set()
        for r in reads:
            w = self.last_w.get(r)
            if w is not None:
                deps.add(w)
        for r in writes:
            w = self.last_w.get(r)
            if w is not None:
                deps.add(w)
            deps.update(self.readers.get(r, ()))
        for r in reads:
            self.readers.setdefault(r, []).append(i)
        for r in writes:
            self.last_w[r] = i
            self.readers[r] = []
        deps.discard(i)
        self.ops.append(dict(eng=eng, fn=fn, deps=deps, dma=dma, sig=None, need=False))
        return i

    def pe(self, fn, reads=(), writes=()):
        return self.add('pe', fn, reads, writes)

    def act(self, fn, reads=(), writes=()):
        return self.add('act', fn, reads, writes)

    def dve(self, fn, reads=(), writes=()):
        return self.add('dve', fn, reads, writes)

    def pool(self, fn, reads=(), writes=()):
        return self.add('pool', fn, reads, writes)

    def dma(self, q, out, in_, reads=(), writes=()):
        return self.add(q, lambda e: e.dma_start(out=out, in_=in_), reads, writes, dma=True)

    def emit(self):
        nc = self.nc
        ops = self.ops
        for op in ops:
            for d in op['deps']:
                a = ops[d]
                if a['eng'] == 'pe' and op['eng'] == 'pe' and not a['dma']:
                    continue
                a['need'] = True
        last_of = {}
        for idx, op in enumerate(ops):
            if not op['dma']:
                last_of[op['eng']] = idx
        for idx in last_of.values():
            ops[idx]['need'] = True

        def new_sem(name):
            h = nc.alloc_semaphore(name=name)
            return h

        ccount = self.ctx.ccount
        dcount = self.ctx.dcount
        dsem = self.ctx.dsem
        for idx, op in enumerate(ops):
            e = op['eng']
            if op['dma']:
                j = dcount.get(e, 0)
                dcount[e] = j + 1
                slot = j % self.dma_sems
                key = (e, slot)
                if key not in dsem or dsem[key][1] + 16 > SEM_EPOCH:
                    dsem[key] = [new_sem("d_%s_%d_%d" % (e, slot, j)), 0]
                    op['prev'] = None
                else:
                    op['prev'] = (dsem[key][0], dsem[key][1])
                dsem[key][1] += 16
                op['sig'] = (dsem[key][0], dsem[key][1])
            elif op['need']:
                if e not in ccount or ccount[e][1] + 1 > SEM_EPOCH:
                    ccount[e] = [new_sem("c_%s_%d" % (e, idx)), 0]
                ccount[e][1] += 1
                op['sig'] = (ccount[e][0], ccount[e][1])
        per_eng = {}
        for idx, op in enumerate(ops):
            per_eng.setdefault(op['eng'], []).append(idx)

        final_waits = [op['sig'] for op in ops if op['dma']]
        final_waits += [ops[idx]['sig'] for idx in last_of.values()]

        def run_engine(ename, eh, is_last_waiter=True):
            observed = {}
            for idx in per_eng.get(ename, []):
                op = ops[idx]
                waits = {}
                for d in op['deps']:
                    a = ops[d]
                    if a['sig'] is None:
                        continue
                    if a['eng'] == 'pe' and ename == 'pe' and not a['dma']:
                        continue
                    s, v = a['sig']
                    k = id(s)
                    if k not in waits or waits[k][1] < v:
                        waits[k] = (s, v)
                if op['dma'] and op.get('prev') is not None:
                    s, v = op['prev']
                    k = id(s)
                    if k not in waits or waits[k][1] < v:
                        waits[k] = (s, v)
                for k, (s, v) in waits.items():
                    if observed.get(k, 0) >= v:
                        continue
                    observed[k] = v
                    eh.wait_ge(s, v)
                ins = op['fn'](eh)
                if op['sig'] is not None:
                    s, v = op['sig']
                    ins.then_inc(s, 16 if op['dma'] else 1)
            if is_last_waiter:
                last = {}
                for s, v in final_waits:
                    k = id(s)
                    if k not in last or last[k][1] < v:
                        last[k] = (s, v)
                for k, (s, v) in last.items():
                    if observed.get(k, 0) >= v:
                        continue
                    eh.wait_ge(s, v)

        with nc.Block() as block:
            @block.tensor
            def _(e):
                run_engine('pe', e)

            @block.scalar
            def _(e):
                run_engine('act', e)

            @block.vector
            def _(e):
                run_engine('dve', e)

            @block.gpsimd
            def _(e):
                run_engine('pool', e)

            @block.sync
            def _(e):
                run_engine('sp', e)


class SemCtx:
    def __init__(self):
        self.ccount = {}
        self.dcount = {}
        self.dsem = {}


import math
from contextlib import ExitStack
import ml_dtypes

L = 4096
D = 2048
NFFT = 8192
ALPHA = (2 * 2) ** 0.25
LN_EPS = 1e-5
TWO_PI = 2.0 * math.pi
MAGIC = 12582912.0
SCALE = 128 ** -0.5


def _consts():
    c = {}
    c["ident_f"] = np.eye(128, dtype=np.float32)
    c["ident_b"] = np.eye(128).astype(ml_dtypes.bfloat16)
    t_norm = np.linspace(0.0, 1.0, L, dtype=np.float32)
    w = (2.0 * math.pi * np.arange(L, dtype=np.float32) / L).astype(np.float32)
    f = np.linspace(1e-4, 15, 16, dtype=np.float32)
    ang = w[:, None] * f[None, :]
    feat = np.concatenate([t_norm[:, None], np.cos(ang), -np.sin(ang)], axis=-1).astype(np.float32)
    c["featT"] = np.ascontiguousarray(feat.T)
    c["tnorm"] = np.ascontiguousarray(t_norm.reshape(32, 128).T)
    min_decay = math.log(1e-2) / 1.5
    max_decay = math.log(1e-2) / 0.3
    deltas = np.abs(np.linspace(min_decay, max_decay, D, dtype=np.float32))
    c["negdelta"] = (-deltas).astype(np.float32)
    n1 = np.arange(32)[:, None, None]
    n2 = np.arange(128)[None, :, None]
    k1 = np.arange(64)[None, None, :]
    n1 = np.arange(64)[:, None, None]
    th = 2 * np.pi * ((128 * n1 + n2) * (k1 + 0.5) % NFFT) / NFFT
    F1 = np.concatenate([np.cos(th), -np.sin(th)], axis=2)
    F1[32:] *= -1.0
    c["F1c"] = F1.astype(ml_dtypes.bfloat16)
    c["Jrev"] = np.eye(128)[::-1].copy().astype(ml_dtypes.bfloat16)
    nn2 = np.arange(128)[:, None]
    k2 = np.arange(64)[None, :]
    ph = 2 * np.pi * ((nn2 * k2) % 128) / 128
    Cr, Ci = np.cos(ph), -np.sin(ph)
    c["Cmat"] = np.stack([Cr, Ci, -Ci, -Cr], axis=1).astype(ml_dtypes.bfloat16)
    ph2 = ph.T
    Er, Ei = 2 * np.cos(ph2), 2 * np.sin(ph2)
    E = np.stack([Er, Ei, -Ei], axis=1)
    c["Emat"] = np.concatenate([E, E], axis=0).astype(ml_dtypes.bfloat16)
    kk = np.arange(64)[:, None, None]
    m2 = np.arange(128)[None, :, None]
    m1 = np.arange(32)[None, None, :]
    th2 = 2 * np.pi * ((128 * m1 + m2) * (kk + 0.5) % NFFT) / NFFT
    T = np.concatenate([np.cos(th2), -np.sin(th2)], axis=0) / NFFT
    c["Tmat"] = T.astype(ml_dtypes.bfloat16)
    slopes = np.exp2(-8.0 * np.arange(1, 17, dtype=np.float32) / 16)
    qi = np.arange(128)[:, None]
    kj = np.arange(384)[None, :]
    dist = np.abs(qi - kj + 128).astype(np.float32)
    ab = -slopes[None, :, None] * dist[:, None, :]
    ab = np.where((dist <= 128)[:, None, :], ab, -30000.0)
    c["abias"] = np.ascontiguousarray(ab.astype(np.float32))
    return c


class K:
    pass


def skew(N, stages):
    K_ = len(stages)
    for t in range(N + K_ - 1):
        for k in reversed(range(K_)):
            n = t - k
            if 0 <= n < N:
                stages[k](n)


_UID = [0]


def _alloc(es, nc, name, shape, dt):
    _UID[0] += 1
    return es.enter_context(nc.sbuf_tensor("%s_%d" % (name, _UID[0]), shape, dt))


def _palloc(es, nc, name, shape, dt):
    _UID[0] += 1
    return es.enter_context(nc.psum_tensor("%s_%d" % (name, _UID[0]), shape, dt))


def ph_cast(nc, ctx, pairs):
    with ExitStack() as es:
        cs = _alloc(es, nc, "cs", [128, 4, 2048], F32)
        cb = _alloc(es, nc, "cb", [128, 4, 2048], BF16)
        S = Sched(nc, ctx)
        jobs = []
        for src, dst, R, C in pairs:
            for r in range(R // 128):
                for c0 in range(0, C, 2048):
                    jobs.append((src, dst, r, c0, min(2048, C - c0)))

        def st_l(n):
            src, dst, r, c0, wd = jobs[n]
            S.dma('sp', cs[:, n % 4, :wd], src[r * 128:(r + 1) * 128, c0:c0 + wd], writes=[('cs', n % 4)])

        def st_c(n):
            src, dst, r, c0, wd = jobs[n]
            k = n % 4
            eng = ('dve', 'act', 'pool')[n % 3]
            if eng == 'act':
                S.act(lambda e: e.copy(cb[:, k, :wd], cs[:, k, :wd]), [('cs', k)], [('cb', k)])
            else:
                S.add(eng, lambda e: e.tensor_copy(cb[:, k, :wd], cs[:, k, :wd]), [('cs', k)], [('cb', k)])

        def st_s(n):
            src, dst, r, c0, wd = jobs[n]
            S.dma('act', dst[r * 128:(r + 1) * 128, c0:c0 + wd], cb[:, n % 4, :wd], reads=[('cb', n % 4)])

        skew(len(jobs), [st_l, st_c, st_s])
        S.emit()


def ph_zero_rows(nc, ctx, g):
    with ExitStack() as es:
        z = _alloc(es, nc, "zr", [2, 8192], F32)
        S = Sched(nc, ctx)
        S.dve(lambda e: e.memset(z[:], 0.0), writes=['z'])
        S.dma('sp', g.Ptm[0:1, :], z[0:1, :], reads=['z'])
        S.dma('sp', g.Ptm[4097:4098, :], z[1:2, :], reads=['z'])
        zb16 = _alloc(es, nc, "zb16", [2, D], BF16)
        S.dve(lambda e: e.memset(zb16[:], 0.0), writes=['zb16'])
        for o in range(2):
            S.dma('sp', g.Gt[o, 4096:4097, :], zb16[o:o + 1, :], reads=['zb16'])
        S.emit()


def ph_filter_mlp(nc, ctx, g):
    with ExitStack() as es:
        ft = _alloc(es, nc, "ft", [33, L], F32)
        w1 = _alloc(es, nc, "w1", [33, 64], F32)
        w23 = _alloc(es, nc, "w23", [64, 2, 64], F32)
        vec = _alloc(es, nc, "vec", [64, 9], F32)
        hA = _alloc(es, nc, "hA", [64, L], F32)
        hB = _alloc(es, nc, "hB", [64, L], F32)
        ta = _alloc(es, nc, "ta", [64, 2, 512], F32)
        tk = _alloc(es, nc, "tk", [64, 2, 512], F32)
        ps = _palloc(es, nc, "ps", [128, 4, 512], F32)
        S = Sched(nc, ctx)
        S.dma('sp', ft[:], g.featT, writes=['ft'])
        S.dma('sp', w1[:], g.w_f1, writes=['w1'])
        S.dma('sp', w23[:, 0, :], g.w_f2, writes=['w23'])
        S.dma('sp', w23[:, 1, :], g.w_f3, writes=['w23'])
        for j, v in enumerate([g.b_f1, g.fr1, g.b_f2, g.fr2, g.b_f3, g.fr3]):
            S.dma('sp', vec[:, j:j + 1], v.rearrange("(p o) -> p o", o=1), writes=['vec'])
        for l in range(3):
            S.dve(lambda e, l=l: e.tensor_tensor(vec[:, 6 + l:7 + l], vec[:, 2 * l:2 * l + 1], vec[:, 2 * l + 1:2 * l + 2], ALU.mult),
                  ['vec'], ['vec'])
        srcs = [ft, hA, hB]
        dsts = [hA, hB, hA]
        names = ['ft', 'hA', 'hB', 'hA']
        for l in range(3):
            src, dst = srcs[l], dsts[l]
            W = w1[:, :] if l == 0 else w23[:, l - 1, :]
            for c in range(8):
                b = c % 4
                s2 = c % 2
                S.pe(lambda e, b=b, W=W, src=src, c=c: e.matmul(ps[0:64, b, :], W, src[:, c * 512:(c + 1) * 512], start=True, stop=True),
                     [names[l], 'w1', 'w23'], [('ps', b)])
                S.dve(lambda e, b=b, s2=s2, l=l: e.tensor_scalar(ta[:, s2, :], ps[0:64, b, :], vec[:, 2 * l + 1:2 * l + 2], vec[:, 6 + l:7 + l], ALU.mult, ALU.add),
                      [('ps', b), 'vec'], [('ta', s2)])
                S.dve(lambda e, s2=s2: e.tensor_scalar(tk[:, s2, :], ta[:, s2, :], 1.0 / TWO_PI, MAGIC, ALU.mult, ALU.add), [('ta', s2)], [('tk', s2)])
                S.dve(lambda e, s2=s2: e.tensor_single_scalar(tk[:, s2, :], tk[:, s2, :], MAGIC, ALU.subtract), [('tk', s2)], [('tk', s2)])
                S.dve(lambda e, s2=s2: e.scalar_tensor_tensor(ta[:, s2, :], tk[:, s2, :], -TWO_PI, ta[:, s2, :], ALU.mult, ALU.add),
                      [('tk', s2), ('ta', s2)], [('ta', s2)])
                S.act(lambda e, s2=s2, dst=dst, c=c: e.activation(dst[:, c * 512:(c + 1) * 512], ta[:, s2, :], AF.Sin), [('ta', s2)], [names[l + 1]])
        S.dma('sp', g.H3, hA[:], reads=['hA'])
        S.emit()


def ph_filter_gen(nc, ctx, g):
    with ExitStack() as es:
        h3 = _alloc(es, nc, "h3", [64, L], F32)
        w4 = _alloc(es, nc, "w4", [64, 8192], F32)
        h3b = _alloc(es, nc, "h3b", [64, L], BF16)
        w4b = _alloc(es, nc, "w4b", [64, 8192], BF16)
        nd = _alloc(es, nc, "nd", [128, D], F32)
        tn = _alloc(es, nc, "tn", [128, 32], F32)
        dec = _alloc(es, nc, "dec", [128, 3, 512], F32)
        kt = _alloc(es, nc, "kt", [128, 4, 512], F32)
        kb = _alloc(es, nc, "kb", [128, 4, 512], BF16)
        sq = _alloc(es, nc, "sq", [128, 4, 512], BF16)
        ones = _alloc(es, nc, "ones", [128, 1], BF16)
        ssr = _alloc(es, nc, "ssr", [1, 8192], F32)
        nr = _alloc(es, nc, "nr", [1, 4096], F32)
        ps = _palloc(es, nc, "ps", [128, 3, 512], F32)
        psr = _palloc(es, nc, "psr", [128, 1, 512], F32)
        pss = _palloc(es, nc, "pss", [128, 4, 512], F32)
        jr = _alloc(es, nc, "jr", [128, 128], BF16)
        kr = _alloc(es, nc, "kr", [128, 2, 512], BF16)
        S = Sched(nc, ctx)
        S.dma('sp', jr[:], g.Jrev, writes=['jr'])
        S.dma('sp', h3[:], g.H3, writes=['h3'])
        S.dma('sp', w4[:], g.w_f4, writes=['w4'])
        S.dma('sp', nd[:], g.negdelta.partition_broadcast(128), writes=['nd'])
        S.dma('sp', tn[:], g.tnorm, writes=['tn'])
        S.dve(lambda e: e.memset(ones[:], 1.0), writes=['ones'])
        S.dve(lambda e: e.tensor_copy(h3b[:], h3[:]), ['h3'], ['h3b'])
        S.act(lambda e: e.copy(w4b[:, 0:4096], w4[:, 0:4096]), ['w4'], ['w4b0'])
        S.pool(lambda e: e.tensor_copy(w4b[:, 4096:8192], w4[:, 4096:8192]), ['w4'], ['w4b1'])

        def dec_n(n):
            cc, r = divmod(n, 128)
            i, od = divmod(r, 4)
            return cc, i, od, (n // 4) % 3, n % 4

        def st0(n):
            cc, i, od, ds, k = dec_n(n)
            if od == 0:
                S.act(lambda e: e.activation(dec[:, ds, :], nd[:, cc * 512:(cc + 1) * 512], AF.Exp, scale=tn[:, i:i + 1]), ['nd', 'tn'], [('dec', ds)])
            col0 = od * 2048 + cc * 512
            S.pe(lambda e: e.matmul(ps[:, n % 3, :], h3b[:, i * 128:(i + 1) * 128], w4b[:, col0:col0 + 512], start=True, stop=True),
                 ['h3b', 'w4b0', 'w4b1'], [('ps', n % 3)])

        def st1(n):
            cc, i, od, ds, k = dec_n(n)
            S.dve(lambda e: e.tensor_tensor(kt[:, k, :], ps[:, n % 3, :], dec[:, ds, :], ALU.mult), [('ps', n % 3), ('dec', ds)], [('kt', k)])
            if od % 2 == 1 and i == 0:
                S.dve(lambda e: e.memset(kt[0:1, k, :], 0.0), [], [('kt', k)])

        def st2(n):
            cc, i, od, ds, k = dec_n(n)
            S.act(lambda e: e.copy(kb[:, k, :], kt[:, k, :]), [('kt', k)], [('kb', k)])
            if od % 2 == 0:
                S.dma('sp', g.Gt[od // 2, i * 128:(i + 1) * 128, cc * 512:(cc + 1) * 512], kb[:, k, :], reads=[('kb', k)])
            else:
                S.pe(lambda e: e.matmul(psr[:, 0, :], jr[:], kb[:, k, :], start=True, stop=True), ['jr', ('kb', k)], ['psr'])
            if n % 2 == 0:
                S.pool(lambda e: e.tensor_tensor(sq[:, k, :], kt[:, k, :], kt[:, k, :], ALU.mult), [('kt', k)], [('sq', k)])
            else:
                S.act(lambda e: e.activation(sq[:, k, :], kt[:, k, :], AF.Square), [('kt', k)], [('sq', k)])

        def st3(n):
            cc, i, od, ds, k = dec_n(n)
            S.pe(lambda e: e.matmul(pss[0:1, od, :], ones[:, 0:1], sq[:, k, :], start=(i == 0), stop=(i == 31)), [('sq', k), 'ones'], [('pss', od)])
            if od % 2 == 1:
                r2 = (n // 2) % 2
                S.dve(lambda e: e.tensor_copy(kr[:, r2, :], psr[:, 0, :]), ['psr'], [('kr', r2)])
                r0 = 8065 - 128 * i
                nrow = 127 if i == 0 else 128
                S.dma('act', g.Gt[od // 2, r0:r0 + nrow, cc * 512:(cc + 1) * 512], kr[0:nrow, r2, :], reads=[('kr', r2)])
            if i == 31:
                col0 = od * 2048 + cc * 512
                S.act(lambda e: e.copy(ssr[0:1, col0:col0 + 512], pss[0:1, od, :]), [('pss', od)], ['ssr'])

        skew(512, [st0, st1, st2, st3])
        for o in range(2):
            S.dve(lambda e, o=o: e.tensor_tensor(nr[0:1, o * 2048:(o + 1) * 2048], ssr[0:1, (2 * o) * 2048:(2 * o + 1) * 2048],
                                                 ssr[0:1, (2 * o + 1) * 2048:(2 * o + 2) * 2048], ALU.add), ['ssr'], ['nr'])
        S.dve(lambda e: e.tensor_scalar_add(nr[:], nr[:], 1e-12), ['nr'], ['nr'])
        S.act(lambda e: e.activation(nr[:], nr[:], AF.Sqrt), ['nr'], ['nr'])
        S.dve(lambda e: e.reciprocal(nr[:], nr[:]), ['nr'], ['nr'])
        S.dma('sp', g.nrm.rearrange("(a o) c -> a (o c)", a=1), nr[:], reads=['nr'])
        S.emit()


def ph_fft_s1(nc, ctx, g, jobs):
    with ExitStack() as es:
        F1 = _alloc(es, nc, "F1", [64, 128, 128], BF16)
        X = _alloc(es, nc, "X", [64, 3, 4, D], BF16)
        A = _alloc(es, nc, "A", [128, 3, 2, D], BF16)
        ps = _palloc(es, nc, "ps", [128, 8, 512], F32)
        S = Sched(nc, ctx)
        S.dma('sp', F1[:], g.F1c, writes=['F1'])
        nx = 0
        na = 0
        nm = 0
        for src, dst in jobs:
            sv = src.rearrange("(n1 n2) c -> n1 n2 c", n2=128)
            KR = src.shape[0] // 128
            for gq in range(32):
                xs = nx % 3
                nx += 1
                S.dma('sp', X[0:KR, xs, :, :], sv[:, gq * 4:(gq + 1) * 4, :], writes=[('X', xs)])
                for jj in range(4):
                    n2 = gq * 4 + jj
                    a = na % 3
                    for cg in range(4):
                        b = nm % 8
                        S.pe(lambda e, b=b, n2=n2, xs=xs, jj=jj, KR=KR, cg=cg: e.matmul(ps[:, b, :], F1[0:KR, n2, :], X[0:KR, xs, jj, cg * 512:(cg + 1) * 512], start=True, stop=True),
                             ['F1', ('X', xs)], [('ps', b)])
                        if nm % 2 == 0:
                            S.act(lambda e, b=b, a=a, jj=jj, cg=cg: e.copy(A[:, a, jj % 2, cg * 512:(cg + 1) * 512], ps[:, b, :]), [('ps', b)], [('A', a, jj % 2, cg)])
                        else:
                            S.dve(lambda e, b=b, a=a, jj=jj, cg=cg: e.tensor_copy(A[:, a, jj % 2, cg * 512:(cg + 1) * 512], ps[:, b, :]), [('ps', b)], [('A', a, jj % 2, cg)])
                        nm += 1
                    if jj % 2 == 1:
                        n20 = n2 - 1
                        S.dma('act', dst[:, n20:n20 + 2, :], A[:, a, :, :], reads=[('A', a, q, c_) for q in range(2) for c_ in range(4)])
                        na += 1
        S.emit()


def ph_filter_s2(nc, ctx, g, o):
    with ExitStack() as es:
        Cm = _alloc(es, nc, "Cm", [128, 4, 64], BF16)
        Ain = _alloc(es, nc, "Ain", [128, 3, 4, 512], BF16)
        nb = _alloc(es, nc, "nb", [128, D], F32)
        ko = _alloc(es, nc, "ko", [128, 3, 2, 512], F32)
        ps = _palloc(es, nc, "ps", [128, 4, 2, 512], F32)
        S = Sched(nc, ctx)
        S.dma('sp', Cm[:], g.Cmat, writes=['Cm'])
        S.dma('sp', nb[:], g.nrm[o].partition_broadcast(128), writes=['nb'])

        def dn(n):
            cg, pair = divmod(n, 32)
            return cg * 512, pair

        def st_load(n):
            c0, pair = dn(n)
            s = n % 3
            for half in range(2):
                k1 = 2 * pair + half
                S.dma('sp', Ain[:, s, half * 2 + 0, :], g.Ascr[k1, :, c0:c0 + 512], writes=[('Ain', s, half * 2 + 0)])
                S.dma('act', Ain[:, s, half * 2 + 1, :], g.Ascr[64 + k1, :, c0:c0 + 512], writes=[('Ain', s, half * 2 + 1)])

        def st_pe(n):
            s = n % 3
            pb = n % 4
            for half in range(2):
                rows = slice(half * 64, (half + 1) * 64)
                for ri, seq in enumerate([[(0, 0), (2, 1)], [(1, 0), (0, 1)]]):
                    for t, (mi, ai) in enumerate(seq):
                        S.pe(lambda e, rows=rows, mi=mi, ai=ai, t=t, ri=ri, half=half: e.matmul(ps[rows, pb, ri, :], Cm[:, mi, :], Ain[:, s, half * 2 + ai, :], start=(t == 0), stop=(t == 1)),
                             ['Cm', ('Ain', s, half * 2 + ai)], [('ps', pb)])

        def st_ev(n):
            c0, pair = dn(n)
            pb = n % 4
            kk = n % 3
            S.dve(lambda e: e.tensor_tensor(ko[:, kk, 0, :], ps[:, pb, 0, :], nb[:, c0:c0 + 512], ALU.mult), [('ps', pb), 'nb'], [('ko', kk, 0)])
            S.dve(lambda e: e.tensor_tensor(ko[:, kk, 1, :], ps[:, pb, 1, :], nb[:, c0:c0 + 512], ALU.mult), [('ps', pb), 'nb'], [('ko', kk, 1)])
            S.dma('act', g.KF[o, pair, :, :, c0:c0 + 512].rearrange("r p c -> p r c"), ko[:, kk, :, :], reads=[('ko', kk, 0), ('ko', kk, 1)])

        skew(128, [st_load, st_pe, st_ev])
        S.emit()


def ph_conv_mid(nc, ctx, g, o):
    with ExitStack() as es:
        Cm = _alloc(es, nc, "Cm", [128, 4, 64], BF16)
        Em = _alloc(es, nc, "Em", [128, 3, 128], BF16)
        Ain = _alloc(es, nc, "Ain", [128, 3, 4, D], BF16)
        Kin = _alloc(es, nc, "Kin", [128, 3, 2, D], F32)
        tt = _alloc(es, nc, "tt", [128, 3, 4, 512], F32)
        Y = _alloc(es, nc, "Y", [128, 3, 2, 512], BF16)
        Bo = _alloc(es, nc, "Bo", [128, 2, 2, 2, D], BF16)
        pz = _palloc(es, nc, "pz", [128, 2, 2, 512], F32)
        pb_ = _palloc(es, nc, "pbb", [128, 2, 2, 512], F32)
        S = Sched(nc, ctx)
        S.dma('sp', Cm[:], g.Cmat, writes=['Cm'])
        S.dma('sp', Em[:], g.Emat, writes=['Em'])
        Bv = g.Bscr.rearrange("n (r k) c -> n r k c", r=2)

        def load_pair(pair):
            s = pair % 3
            S.dma('act', Kin[:, s, :, :], g.KF[o, pair].rearrange("r p c -> p r c"), writes=[('Kin', s)])
            for half in range(2):
                k1 = 2 * pair + half
                S.dma('sp', Ain[:, s, 2 * half, :], g.Ascr[k1], writes=[('Ain', s, 2 * half)])
                S.dma('sp' if half == 0 else 'act', Ain[:, s, 2 * half + 1, :], g.Ascr[64 + k1], writes=[('Ain', s, 2 * half + 1)])

        def st_load(n):
            pair, cg = divmod(n, 4)
            if cg == 0 and pair + 1 < 32:
                load_pair(pair + 1)

        def st_z(n):
            pair, cg = divmod(n, 4)
            s = pair % 3
            cs = slice(cg * 512, (cg + 1) * 512)
            zs = n % 2
            for half in range(2):
                rows = slice(half * 64, (half + 1) * 64)
                for ri, seq in enumerate([[(0, 0), (2, 1)], [(1, 0), (0, 1)]]):
                    for t, (mi, ai) in enumerate(seq):
                        S.pe(lambda e, rows=rows, mi=mi, ai=ai, t=t, ri=ri, half=half: e.matmul(pz[rows, zs, ri, :], Cm[:, mi, :], Ain[:, s, 2 * half + ai, cs], start=(t == 0), stop=(t == 1)),
                             ['Cm', ('Ain', s, 2 * half + ai)], [('pz', zs)])

        def st_m(n):
            pair, cg = divmod(n, 4)
            s = pair % 3
            cs = slice(cg * 512, (cg + 1) * 512)
            zs = n % 2
            q = n % 3
            for t, (zi, ki) in enumerate([(0, 0), (1, 1), (0, 1), (1, 0)]):
                S.dve(lambda e, t=t, zi=zi, ki=ki: e.tensor_tensor(tt[:, q, t, :], pz[:, zs, zi, :], Kin[:, s, ki, cs], ALU.mult),
                      [('pz', zs), ('Kin', s)], [('tt', q, t)])
            S.pool(lambda e: e.tensor_tensor(Y[:, q, 0, :], tt[:, q, 0, :], tt[:, q, 1, :], ALU.subtract), [('tt', q, 0), ('tt', q, 1)], [('Y', q, 0)])
            S.pool(lambda e: e.tensor_tensor(Y[:, q, 1, :], tt[:, q, 2, :], tt[:, q, 3, :], ALU.add), [('tt', q, 2), ('tt', q, 3)], [('Y', q, 1)])

        def st_b(n):
            pair, cg = divmod(n, 4)
            s = pair % 2
            cs = slice(cg * 512, (cg + 1) * 512)
            q = n % 3
            for half in range(2):
                rows = slice(half * 64, (half + 1) * 64)
                for ri, seq in enumerate([[(0, 0), (2, 1)], [(1, 0), (0, 1)]]):
                    for t, (mi, yi) in enumerate(seq):
                        S.pe(lambda e, half=half, ri=ri, rows=rows, mi=mi, yi=yi, t=t: e.matmul(pb_[:, half, ri, :], Em[rows, mi, :], Y[rows, q, yi, :], start=(t == 0), stop=(t == 1)),
                             ['Em', ('Y', q, yi)], [('pb', half, ri)])
                    S.act(lambda e, half=half, ri=ri: e.copy(Bo[:, s, ri, half, cs], pb_[:, half, ri, :]), [('pb', half, ri)], [('Bo', s, ri, half, cg)])

        def st_o(n):
            pair, cg = divmod(n, 4)
            if cg != 3:
                return
            s = pair % 2
            S.dma('act', Bv[:, :, 2 * pair:2 * pair + 2, :], Bo[:, s, :, :, :],
                  reads=[('Bo', s, r_, h_, c_) for r_ in range(2) for h_ in range(2) for c_ in range(4)])

        load_pair(0)
        skew(128, [st_load, st_z, st_m, st_b, st_o])
        S.emit()


def ph_conv_is2(nc, ctx, g, o):
    with ExitStack() as es:
        Tm = _alloc(es, nc, "Tm", [128, 128, 32], BF16)
        hb = _alloc(es, nc, "hb", [128, D], F32)
        Bin = _alloc(es, nc, "Bin", [128, 3, 3, D], BF16)
        zin = _alloc(es, nc, "zin", [128, 3, D], F32)
        gin = _alloc(es, nc, "gin", [128, 3, D], F32)
        sg = _alloc(es, nc, "sg", [128, 3 if o == 1 else 1, D], F32)
        t1 = _alloc(es, nc, "t1", [128, 3, D], F32)
        ob = _alloc(es, nc, "ob", [128, 3, D], BF16)
        ps = _palloc(es, nc, "ps", [128, 8, 512], F32)
        S = Sched(nc, ctx)
        S.dma('sp', Tm[:], g.Tmat, writes=['Tm'])
        S.dma('sp', hb[:], g.h_bias[o].partition_broadcast(128), writes=['hb'])
        for nm, tl in [('zin', zin), ('gin', gin), ('sg', sg)]:
            S.dve(lambda e, tl=tl: e.memset(tl[:], 0.0), [], [(nm, s_, h_) for s_ in range(3) for h_ in range(3)])
        zsrc = (g.U0 if o == 0 else g.Z1).rearrange("(n1 n2) c -> n1 n2 c", n2=128)
        gsrc = (g.U1 if o == 0 else g.U2).rearrange("(n1 n2) c -> n1 n2 c", n2=128)
        ssrc = g.Ptm[1:4097, :].rearrange("(n1 n2) c -> n1 n2 c", n2=128)
        z1d = g.Z1.rearrange("(n1 n2) c -> n1 n2 c", n2=128)
        ztd = g.zT.rearrange("(n1 n2) c -> n1 n2 c", n2=128)
        ytd = g.Ytm.rearrange("(n1 n2) c -> n1 n2 c", n2=128)
        NIT = 43

        def hfs(pr):
            return [(hf, 3 * pr + hf, slice(hf * 32, hf * 32 + 32)) for hf in range(3) if 3 * pr + hf < 128]

        def st_load(pr):
            s = pr % 3
            for hf, n2, rows in hfs(pr):
                S.dma('sp', Bin[:, s, hf, :], g.Bscr[n2, :, :], writes=[('Bin', s, hf)])
                S.dma('sp', zin[rows, s, :], zsrc[:, n2, :], writes=[('zin', s, hf)])
                S.dma('act', gin[rows, s, :], gsrc[:, n2, :], writes=[('gin', s, hf)])
                if o == 1:
                    S.dma('act', sg[rows, s, :], ssrc[:, n2, 6144:8192], writes=[('sg', s, hf)])

        def st_pe(pr):
            s = pr % 3
            for cg in range(4):
                b = (pr * 4 + cg) % 8
                for hf, n2, rows in hfs(pr):
                    S.pe(lambda e, b=b, rows=rows, n2=n2, hf=hf, cg=cg: e.matmul(ps[rows, b, :], Tm[:, n2, :], Bin[:, s, hf, cg * 512:(cg + 1) * 512], start=True, stop=True),
                         ['Tm', ('Bin', s, hf)], [('ps', b)])

        def zk(nm, s):
            return [(nm, s, h_) for h_ in range(3)]

        def st_e1(pr):
            s = pr % 3
            for cg in range(4):
                b = (pr * 4 + cg) % 8
                cs = slice(cg * 512, (cg + 1) * 512)
                S.pool(lambda e, cs=cs: e.tensor_tensor(t1[:, s, cs], zin[:, s, cs], hb[:, cs], ALU.mult), zk('zin', s) + ['hb'], [('t1', s, cg)])
                S.dve(lambda e, b=b, cs=cs: e.tensor_tensor(t1[:, s, cs], t1[:, s, cs], ps[:, b, :], ALU.add), [('t1', s, cg), ('ps', b)], [('t1', s, cg)])
                S.dve(lambda e, cs=cs: e.tensor_tensor(t1[:, s, cs], t1[:, s, cs], gin[:, s, cs], ALU.mult), [('t1', s, cg)] + zk('gin', s), [('t1', s, cg)])
                if o == 1:
                    S.dve(lambda e, cs=cs: e.tensor_tensor(ob[:, s, cs], t1[:, s, cs], sg[:, s, cs], ALU.mult), [('t1', s, cg)] + zk('sg', s), [('ob', s, cg)])

        def st_e2(pr):
            s = pr % 3
            t1k = [('t1', s, cg) for cg in range(4)]
            obk = [('ob', s, c_) for c_ in range(4)]
            if o == 0:
                S.act(lambda e: e.copy(ob[:, s, :], t1[:, s, :]), t1k, obk)
                for hf, n2, rows in hfs(pr):
                    S.dma('act', z1d[:, n2, :], t1[rows, s, :], reads=t1k)
                    S.dma('sp', ztd[:, n2, :], ob[rows, s, :], reads=obk)
            else:
                for hf, n2, rows in hfs(pr):
                    S.dma('sp', ytd[:, n2, :], ob[rows, s, :], reads=obk)

        skew(NIT, [st_load, st_pe, st_e1, st_e2])
        S.emit()


def ph_inproj(nc, ctx, g, xsrc, wb, ncol, mode):
    with ExitStack() as es:
        xT = _alloc(es, nc, "xT", [128, 16, L], BF16)
        idf = _alloc(es, nc, "idf", [128, 128], F32)
        xs = _alloc(es, nc, "xs", [128, 2, D], F32)
        wc = _alloc(es, nc, "wc", [128, 2, 16, 512], BF16)
        bb = _alloc(es, nc, "bb", [128, 2, 512], F32)
        ot = _alloc(es, nc, "ot", [128, 3, 512], F32)
        ob = _alloc(es, nc, "ob", [128, 3, 512], BF16)
        ps = _palloc(es, nc, "ps", [128, 8, 512], F32)
        S = Sched(nc, ctx)
        S.dma('sp', idf[:], g.ident_f, writes=['idf'])
        nb = 0
        for i in range(32):
            s = i % 2
            S.dma('sp', xs[:, s, :], xsrc[i * 128:(i + 1) * 128, :], writes=[('xs', s)])
            for q in range(4):
                b = nb % 8
                nb += 1
                for j in range(4):
                    kc = q * 4 + j
                    S.pe(lambda e, b=b, j=j, s=s, kc=kc: e.transpose(ps[:, b, j * 128:(j + 1) * 128], xs[:, s, kc * 128:(kc + 1) * 128], idf[:]),
                         [('xs', s), 'idf'], [('ps', b)])
                dst = xT[:, q * 4:(q + 1) * 4, i * 128:(i + 1) * 128]
                src = ps[:, b, :].rearrange("p (j t) -> p j t", j=4)
                if q % 2 == 0:
                    S.act(lambda e, dst=dst, src=src: e.copy(dst, src), [('ps', b)], [('xT', i, q)])
                else:
                    S.dve(lambda e, dst=dst, src=src: e.tensor_copy(dst, src), [('ps', b)], [('xT', i, q)])
        allx = [('xT', i, q) for i in range(32) for q in range(4)]
        wv = wb.rearrange("(kc p) c -> p kc c", p=128)
        no = 0
        for j in range(ncol // 512):
            ws = j % 2
            S.dma('sp', wc[:, ws, :, :], wv[:, :, j * 512:(j + 1) * 512], writes=[('wc', ws)])
            if mode == 'hy':
                S.dma('sp', bb[:, ws, :], g.b_hin[j * 512:(j + 1) * 512].partition_broadcast(128), writes=[('bb', ws)])
            feat_major = (mode == 'at' and j < 5)
            if not feat_major:
                for i in range(32):
                    b = nb % 8
                    nb += 1
                    o = no % 3
                    no += 1
                    for kc in range(16):
                        S.pe(lambda e, b=b, kc=kc, i=i, ws=ws: e.matmul(ps[:, b, :], xT[:, kc, i * 128:(i + 1) * 128], wc[:, ws, kc, :], start=(kc == 0), stop=(kc == 15)),
                             allx[i * 4:(i + 1) * 4] + [('wc', ws)], [('ps', b)])
                    rows = slice(i * 128, (i + 1) * 128)
                    if mode == 'hy':
                        S.dve(lambda e, b=b, o=o, ws=ws: e.tensor_tensor(ot[:, o, :], ps[:, b, :], bb[:, ws, :], ALU.add), [('ps', b), ('bb', ws)], [('ot', o)])
                        if j >= 12:
                            S.act(lambda e, o=o: e.activation(ot[:, o, :], ot[:, o, :], AF.Silu), [('ot', o)], [('ot', o)])
                        S.dma('act', g.Ptm[1 + i * 128:1 + (i + 1) * 128, j * 512:(j + 1) * 512], ot[:, o, :], reads=[('ot', o)])
                    elif j == 5:
                        S.act(lambda e, b=b, o=o: e.copy(ob[:, o, :], ps[:, b, :]), [('ps', b)], [('ob', o)])
                        S.dma('act', g.Vs[:, rows, :].rearrange("k t d -> t k d"), ob[:, o, :].rearrange("p (k d) -> p k d", k=4), reads=[('ob', o)])
                    else:
                        S.act(lambda e, b=b, o=o: e.activation(ot[:, o, :], ps[:, b, :], AF.Silu), [('ps', b)], [('ot', o)])
                        S.dma('act', g.Gtm[rows, (j - 6) * 512:(j - 5) * 512], ot[:, o, :], reads=[('ot', o)])
            else:
                for f in range(4):
                    hh = j * 4 + f
                    for tcn in range(8):
                        b = nb % 8
                        nb += 1
                        o = no % 3
                        no += 1
                        for kc in range(16):
                            S.pe(lambda e, b=b, kc=kc, ws=ws, f=f, tcn=tcn: e.matmul(ps[:, b, :], wc[:, ws, kc, f * 128:(f + 1) * 128], xT[:, kc, tcn * 512:(tcn + 1) * 512], start=(kc == 0), stop=(kc == 15)),
                                 allx[tcn * 16:(tcn + 1) * 16] + [('wc', ws)], [('ps', b)])
                        if no % 2 == 0:
                            S.act(lambda e, b=b, o=o: e.copy(ob[:, o, :], ps[:, b, :]), [('ps', b)], [('ob', o)])
                        else:
                            S.dve(lambda e, b=b, o=o: e.tensor_copy(ob[:, o, :], ps[:, b, :]), [('ps', b)], [('ob', o)])
                        dstq = g.QT[hh, :, tcn * 512:(tcn + 1) * 512] if hh < 16 else g.KT[hh - 16, :, tcn * 512:(tcn + 1) * 512]
                        S.dma('act', dstq, ob[:, o, :], reads=[('ob', o)])
        S.emit()


def ph_inproj_hy2(nc, ctx, g, xsrc):
    with ExitStack() as es:
        xT = _alloc(es, nc, "xT", [128, 16, L + 2], BF16)
        idf = _alloc(es, nc, "idf", [128, 128], F32)
        xs = _alloc(es, nc, "xs", [128, 2, D], F32)
        wc = _alloc(es, nc, "wc", [128, 2, 16, 512], BF16)
        hv = _alloc(es, nc, "hv", [128, 9, 48], F32)
        ub = _alloc(es, nc, "ub", [128, 8, 256], F32)
        ut = _alloc(es, nc, "ut", [128, 4, 512], F32)
        utb = _alloc(es, nc, "utb", [128, 2, 512], BF16)
        bb = _alloc(es, nc, "bb", [128, 2, 512], F32)
        ot = _alloc(es, nc, "ot", [128, 2, 512], F32)
        ps = _palloc(es, nc, "ps", [128, 8, 512], F32)
        S = Sched(nc, ctx)
        S.dma('sp', idf[:], g.ident_f, writes=['idf'])
        S.dma('act', hv[:, 0:5, :], g.hvec, writes=['hv'])
        S.dve(lambda e: e.tensor_tensor(hv[:, 5, :], hv[:, 0, :], hv[:, 1, :], ALU.add), ['hv'], ['hv'])
        S.dve(lambda e: e.tensor_tensor(hv[:, 5, :], hv[:, 5, :], hv[:, 2, :], ALU.add), ['hv'], ['hv'])
        S.dve(lambda e: e.tensor_tensor(hv[:, 6, :], hv[:, 5, :], hv[:, 4, :], ALU.mult), ['hv'], ['hv'])
        S.dve(lambda e: e.tensor_tensor(hv[:, 6, :], hv[:, 6, :], hv[:, 3, :], ALU.add), ['hv'], ['hv'])
        S.dve(lambda e: e.tensor_tensor(hv[:, 7, :], hv[:, 0, :], hv[:, 4, :], ALU.mult), ['hv'], ['hv'])
        S.dve(lambda e: e.tensor_tensor(hv[:, 8, :], hv[:, 2, :], hv[:, 4, :], ALU.mult), ['hv'], ['hv'])
        S.dve(lambda e: e.memset(xT[:, :, 0:1], 0.0), [], ['xTpad0'])
        S.dve(lambda e: e.memset(xT[:, :, L + 1:L + 2], 0.0), [], ['xTpad1'])
        nb = 0
        for i in range(32):
            s = i % 2
            S.dma('sp', xs[:, s, :], xsrc[i * 128:(i + 1) * 128, :], writes=[('xs', s)])
            for q in range(4):
                b = nb % 8
                nb += 1
                for j in range(4):
                    kc = q * 4 + j
                    S.pe(lambda e, b=b, j=j, s=s, kc=kc: e.transpose(ps[:, b, j * 128:(j + 1) * 128], xs[:, s, kc * 128:(kc + 1) * 128], idf[:]),
                         [('xs', s), 'idf'], [('ps', b)])
                dst = xT[:, q * 4:(q + 1) * 4, 1 + i * 128:1 + (i + 1) * 128]
                src = ps[:, b, :].rearrange("p (j t) -> p j t", j=4)
                if q % 2 == 0:
                    S.act(lambda e, dst=dst, src=src: e.copy(dst, src), [('ps', b)], [('xT', i, q)])
                else:
                    S.dve(lambda e, dst=dst, src=src: e.tensor_copy(dst, src), [('ps', b)], [('xT', i, q)])
        allx = [('xT', i, q) for i in range(32) for q in range(4)] + ['xTpad0', 'xTpad1']
        wv = g.wb_hin.rearrange("(kc p) c -> p kc c", p=128)
        outs = [g.U0, g.U1, g.U2]

        def dn(n):
            j, tc = divmod(n, 16)
            return j, tc, j // 4, (j % 4) * 512, j % 2

        def st_w(n):
            j, tc, q, c0, ws = dn(n)
            if tc == 0:
                S.dma('sp', wc[:, ws, :, :], wv[:, :, j * 512:(j + 1) * 512], writes=[('wc', ws)])

        def st_mm(n):
            j, tc, q, c0, ws = dn(n)
            t0 = tc * 256
            for f in range(4):
                for kc in range(16):
                    S.pe(lambda e, f=f, kc=kc: e.matmul(ps[:, f, 0:258], wc[:, ws, kc, f * 128:(f + 1) * 128], xT[:, kc, t0:t0 + 258], start=(kc == 0), stop=(kc == 15)),
                         allx + [('wc', ws)], [('psm', f)])

        def st_conv(n):
            j, tc, q, c0, ws = dn(n)
            for f in range(4):
                fb = j * 4 + f
                u = ub[:, (n % 2) * 4 + f, :]
                uk = ('ub', (n % 2) * 4 + f)
                S.act(lambda e, f=f, fb=fb, u=u: e.activation(u, ps[:, f, 0:256], AF.Identity, bias=hv[:, 6, fb:fb + 1], scale=hv[:, 0, fb:fb + 1]), [('psm', f), 'hv'], [uk])
                S.dve(lambda e, f=f, fb=fb, u=u: e.scalar_tensor_tensor(u, ps[:, f, 1:257], hv[:, 1, fb:fb + 1], u, ALU.mult, ALU.add), [('psm', f), 'hv', uk], [uk])
                S.dve(lambda e, f=f, fb=fb, u=u: e.scalar_tensor_tensor(u, ps[:, f, 2:258], hv[:, 2, fb:fb + 1], u, ALU.mult, ALU.add), [('psm', f), 'hv', uk], [uk])
                if tc == 0:
                    S.dve(lambda e, fb=fb, u=u: e.tensor_tensor(u[:, 0:1], u[:, 0:1], hv[:, 7, fb:fb + 1], ALU.subtract), [uk, 'hv'], [uk])
                if tc == 15:
                    S.dve(lambda e, fb=fb, u=u: e.tensor_tensor(u[:, 255:256], u[:, 255:256], hv[:, 8, fb:fb + 1], ALU.subtract), [uk, 'hv'], [uk])

        def st_tr(n):
            for sub in range(2):
                for f in range(4):
                    S.pe(lambda e, sub=sub, f=f: e.transpose(ps[:, 4 + sub, f * 128:(f + 1) * 128], ub[:, (n % 2) * 4 + f, sub * 128:(sub + 1) * 128], idf[:]),
                         [('ub', (n % 2) * 4 + f), 'idf'], [('pst', sub)])

        def st_ev(n):
            j, tc, q, c0, ws = dn(n)
            for sub in range(2):
                us = (n % 2) * 2 + sub
                r0 = tc * 256 + sub * 128
                if sub == 0:
                    S.act(lambda e, us=us, sub=sub: e.copy(ut[:, us, :], ps[:, 4 + sub, :]), [('pst', sub)], [('ut', us)])
                else:
                    S.dve(lambda e, us=us, sub=sub: e.tensor_copy(ut[:, us, :], ps[:, 4 + sub, :]), [('pst', sub)], [('ut', us)])
                S.dma('act', outs[q][r0:r0 + 128, c0:c0 + 512], ut[:, us, :], reads=[('ut', us)])
                if q == 0:
                    S.pool(lambda e, us=us, sub=sub: e.tensor_copy(utb[:, sub, :], ut[:, us, :]), [('ut', us)], [('utb', sub)])
                    S.dma('sp', g.zT[r0:r0 + 128, c0:c0 + 512], utb[:, sub, :], reads=[('utb', sub)])

        skew(192, [st_w, st_mm, st_conv, st_tr, st_ev])

        no = 0
        for j in range(12, 16):
            ws = j % 2
            S.dma('sp', wc[:, ws, :, :], wv[:, :, j * 512:(j + 1) * 512], writes=[('wc', ws)])
            S.dma('sp', bb[:, ws, :], g.b_hin[j * 512:(j + 1) * 512].partition_broadcast(128), writes=[('bb', ws)])
            for i in range(32):
                b = 6 + (no % 2)
                o = no % 2
                no += 1
                for kc in range(16):
                    S.pe(lambda e, b=b, kc=kc, i=i, ws=ws: e.matmul(ps[:, b, :], xT[:, kc, 1 + i * 128:1 + (i + 1) * 128], wc[:, ws, kc, :], start=(kc == 0), stop=(kc == 15)),
                         allx + [('wc', ws)], [('psg', b)])
                S.dve(lambda e, b=b, o=o, ws=ws: e.tensor_tensor(ot[:, o, :], ps[:, b, :], bb[:, ws, :], ALU.add), [('psg', b), ('bb', ws)], [('ot', o)])
                S.act(lambda e, o=o: e.activation(ot[:, o, :], ot[:, o, :], AF.Silu), [('ot', o)], [('ot', o)])
                S.dma('act', g.Ptm[1 + i * 128:1 + (i + 1) * 128, j * 512:(j + 1) * 512], ot[:, o, :], reads=[('ot', o)])
        S.emit()


def ph_sconv(nc, ctx, g):
    with ExitStack() as es:
        idf = _alloc(es, nc, "idf", [128, 128], F32)
        wbc = _alloc(es, nc, "wbc", [128, 2, 12, 512], F32)
        pin = _alloc(es, nc, "pin", [128, 2, 9, 512], F32)
        prd = _alloc(es, nc, "prd", [128, 2, 9, 512], F32)
        ou = _alloc(es, nc, "ou", [128, 2, 3, 512], F32)
        zb = _alloc(es, nc, "zb", [128, 2, 512], BF16)
        ps = _palloc(es, nc, "ps", [128, 2, 3, 512], F32)
        S = Sched(nc, ctx)
        S.dma('sp', idf[:], g.ident_f, writes=['idf'])
        outs = [g.U0, g.U1, g.U2]

        def st_load(n):
            cg, i = divmod(n, 32)
            c0 = cg * 512
            s = n % 2
            if i == 0:
                ws = cg % 2
                for q in range(3):
                    for j in range(3):
                        S.dma('act', wbc[:, ws, q * 4 + j, :], g.w_sc[j, q * 2048 + c0:q * 2048 + c0 + 512].partition_broadcast(128), writes=[('wbc', ws, q * 4 + j)])
                    S.dma('act', wbc[:, ws, q * 4 + 3, :], g.b_sc[q * 2048 + c0:q * 2048 + c0 + 512].partition_broadcast(128), writes=[('wbc', ws, q * 4 + 3)])
            for q in range(3):
                for j in range(3):
                    S.dma('sp', pin[:, s, q * 3 + j, :], g.Ptm[i * 128 + j:i * 128 + j + 128, q * 2048 + c0:q * 2048 + c0 + 512], writes=[('pin', s, q * 3 + j)])

        def st_mul(n):
            cg, i = divmod(n, 32)
            s = n % 2
            ws = cg % 2
            for m in range(9):
                q, j = divmod(m, 3)
                eng = 'dve' if m in (0, 2, 4, 6) else 'pool'
                S.add(eng, lambda e, m=m, q=q, j=j: e.tensor_tensor(prd[:, s, m, :], pin[:, s, m, :], wbc[:, ws, q * 4 + j, :], ALU.mult),
                      [('pin', s, m), ('wbc', ws, q * 4 + j)], [('prd', s, m)])

        def st_pe(n):
            s = n % 2
            for q in range(3):
                for j in range(3):
                    S.pe(lambda e, q=q, j=j: e.matmul(ps[:, s, q, :], idf[:], prd[:, s, q * 3 + j, :], start=(j == 0), stop=(j == 2)),
                         ['idf', ('prd', s, q * 3 + j)], [('ps', s, q)])

        def st_out(n):
            cg, i = divmod(n, 32)
            c0 = cg * 512
            s = n % 2
            ws = cg % 2
            for q in range(3):
                S.dve(lambda e, q=q: e.tensor_tensor(ou[:, s, q, :], ps[:, s, q, :], wbc[:, ws, q * 4 + 3, :], ALU.add), [('ps', s, q), ('wbc', ws, q * 4 + 3)], [('ou', s, q)])
                S.dma('act', outs[q][i * 128:(i + 1) * 128, c0:c0 + 512], ou[:, s, q, :], reads=[('ou', s, q)])
                if q == 0:
                    S.act(lambda e: e.copy(zb[:, s, :], ou[:, s, 0, :]), [('ou', s, 0)], [('zb', s)])
                    S.dma('act', g.zT[i * 128:(i + 1) * 128, c0:c0 + 512], zb[:, s, :], reads=[('zb', s)])

        skew(128, [st_load, st_mul, st_pe, st_out])
        S.emit()


def ph_outproj(nc, ctx, g, ysrc, xsrc, wb, bias, lg, lb, dst):
    with ExitStack() as es:
        W = _alloc(es, nc, "W", [128, 16, D], BF16)
        idb = _alloc(es, nc, "idb", [128, 128], BF16)
        yt = _alloc(es, nc, "yt", [128, 2, D], BF16)
        yT = _alloc(es, nc, "yT", [128, 2, 16, 128], BF16)
        xr = _alloc(es, nc, "xr", [128, 3, D], F32)
        sm = _alloc(es, nc, "sm", [128, 4, D], F32)
        oo = _alloc(es, nc, "oo", [128, 2, D], F32)
        jk = _alloc(es, nc, "jk", [128, D], BF16)
        gb = _alloc(es, nc, "gb", [128, 3, D], F32)
        st = _alloc(es, nc, "st", [128, 4, 8], F32)
        pt = _palloc(es, nc, "pt", [128, 16, 128], BF16)
        ps = _palloc(es, nc, "ps", [128, 4, 512], F32)
        S = Sched(nc, ctx)
        S.dma('sp', W[:], wb.rearrange("(kc p) c -> p kc c", p=128), writes=['W'])
        S.dma('sp', idb[:], g.ident_b, writes=['idb'])
        S.dma('sp', gb[:, 0, :], lg.partition_broadcast(128), writes=['gb'])
        S.dma('sp', gb[:, 1, :], lb.partition_broadcast(128), writes=['gb'])
        if bias is not None:
            S.dma('sp', gb[:, 2, :], bias.partition_broadcast(128), writes=['gb'])

        def rows(i):
            return slice(i * 128, (i + 1) * 128)

        def st_l(i):
            S.dma('sp', yt[:, i % 2, :], ysrc[rows(i), :], writes=[('yt', i % 2)])

        def st_t(i):
            s = i % 2
            for kc in range(16):
                S.pe(lambda e, kc=kc: e.transpose(pt[:, kc, :], yt[:, s, kc * 128:(kc + 1) * 128], idb[:]), [('yt', s), 'idb'], ['pt'])

        def st_e(i):
            s = i % 2
            S.dma('act', xr[:, i % 3, :], xsrc[rows(i), :], writes=[('xr', i % 3)])
            S.act(lambda e: e.copy(yT[:, s, 0:8, :], pt[:, 0:8, :]), ['pt'], [('yT', s, 0)])
            S.dve(lambda e: e.tensor_copy(yT[:, s, 8:16, :], pt[:, 8:16, :]), ['pt'], [('yT', s, 1)])

        def st_m(i):
            s = i % 2
            for c in range(4):
                for kc in range(16):
                    S.pe(lambda e, c=c, kc=kc: e.matmul(ps[:, c, :], yT[:, s, kc, :], W[:, kc, c * 512:(c + 1) * 512], start=(kc == 0), stop=(kc == 15)),
                         [('yT', s, 0), ('yT', s, 1), 'W'], [('ps', c)])

        def smk(i):
            return [('sm', i % 4, c) for c in range(4)]

        def st_r(i):
            s4, s3 = i % 4, i % 3
            for c in range(4):
                cs = slice(c * 512, (c + 1) * 512)
                if bias is not None:
                    S.dve(lambda e, c=c, cs=cs: e.tensor_tensor(sm[:, s4, cs], ps[:, c, :], gb[:, 2, cs], ALU.add), [('ps', c), 'gb'], [('sm', s4, c)])
                    S.dve(lambda e, cs=cs: e.scalar_tensor_tensor(sm[:, s4, cs], xr[:, s3, cs], ALPHA, sm[:, s4, cs], ALU.mult, ALU.add),
                          [('xr', s3), ('sm', s4, c)], [('sm', s4, c)])
                else:
                    S.dve(lambda e, c=c, cs=cs: e.scalar_tensor_tensor(sm[:, s4, cs], xr[:, s3, cs], ALPHA, ps[:, c, :], ALU.mult, ALU.add),
                          [('xr', s3), ('ps', c)], [('sm', s4, c)])

        def st_a1(i):
            s4 = i % 4
            S.act(lambda e: e.activation(jk[:], sm[:, s4, :], AF.Identity, accum_out=st[:, s4, 0:1]), smk(i), ['jk', ('st', s4, 0)])

        def st_v1(i):
            s4 = i % 4
            S.dve(lambda e: e.tensor_single_scalar(st[:, s4, 1:2], st[:, s4, 0:1], -1.0 / D, ALU.mult), [('st', s4, 0)], [('st', s4, 1)])

        def st_a2(i):
            s4 = i % 4
            S.act(lambda e: e.activation(jk[:], sm[:, s4, :], AF.Square, bias=st[:, s4, 1:2], accum_out=st[:, s4, 2:3]), smk(i) + [('st', s4, 1)], ['jk', ('st', s4, 2)])

        def st_v2(i):
            s4 = i % 4
            S.dve(lambda e: e.tensor_scalar(st[:, s4, 3:4], st[:, s4, 2:3], 1.0 / D, LN_EPS, ALU.mult, ALU.add), [('st', s4, 2)], [('st', s4, 2)])

        def st_a3(i):
            s4 = i % 4
            S.act(lambda e: e.activation(st[:, s4, 4:5], st[:, s4, 3:4], AF.Sqrt), [('st', s4, 2)], [('st', s4, 3)])

        def st_n(i):
            s4, s2 = i % 4, i % 2
            S.dve(lambda e: e.reciprocal(st[:, s4, 5:6], st[:, s4, 4:5]), [('st', s4, 3)], [('st', s4, 3)])
            S.dve(lambda e: e.tensor_scalar(oo[:, s2, :], sm[:, s4, :], st[:, s4, 1:2], st[:, s4, 5:6], ALU.add, ALU.mult),
                  smk(i) + [('st', s4, 1), ('st', s4, 3)], [('oo', s2)])

        def st_g(i):
            s2 = i % 2
            S.pool(lambda e: e.tensor_tensor(oo[:, s2, :], oo[:, s2, :], gb[:, 0, :], ALU.mult), [('oo', s2), 'gb'], [('oo', s2)])
            S.pool(lambda e: e.tensor_tensor(oo[:, s2, :], oo[:, s2, :], gb[:, 1, :], ALU.add), [('oo', s2), 'gb'], [('oo', s2)])
            S.dma('act', dst[rows(i), :], oo[:, s2, :], reads=[('oo', s2)])

        def st_va(i):
            st_v1(i)
            st_a2(i)

        def st_vn(i):
            st_v2(i)
            st_a3(i)
            st_n(i)

        skew(32, [st_l, st_t, st_e, st_m, st_r, st_a1, st_va, st_vn, st_g])
        S.emit()


def ph_attn(nc, ctx, g):
    with ExitStack() as es:
        idb = _alloc(es, nc, "idb", [128, 128], BF16)
        ab = _alloc(es, nc, "ab", [128, 16, 384], F32)
        sk = _alloc(es, nc, "sk", [128, 16], F32)
        kt = _alloc(es, nc, "kt", [128, 2, L], BF16)
        vv = _alloc(es, nc, "vv", [128, 2, 32, 128], BF16)
        qt = _alloc(es, nc, "qt", [128, 2, L], BF16)
        gg = _alloc(es, nc, "gg", [128, 2, 32, 128], F32)
        yh = _alloc(es, nc, "yh", [128, 2, 32, 128], BF16)
        lgt = _alloc(es, nc, "lgt", [128, 3, 384], F32)
        pe_ = _alloc(es, nc, "pe_", [128, 3, 384], F32)
        pn = _alloc(es, nc, "pn", [128, 3, 384], BF16)
        pT = _alloc(es, nc, "pT", [128, 3, 3, 128], BF16)
        sc = _alloc(es, nc, "sc", [128, 6, 8], F32)
        pss = _palloc(es, nc, "pss", [128, 3, 512], F32)
        ptt = _palloc(es, nc, "ptt", [128, 2, 8, 128], BF16)
        po = _palloc(es, nc, "po", [128, 3, 512], F32)
        S = Sched(nc, ctx)
        S.dma('sp', idb[:], g.ident_b, writes=['idb'])
        S.dma('sp', ab[:], g.abias, writes=['ab'])
        S.dma('sp', sk[:], g.sink.partition_broadcast(128), writes=['sk'])
        yv = g.Ytm.rearrange("(i p) c -> p i c", p=128)
        gv = g.Gtm.rearrange("(i p) c -> p i c", p=128)

        def load_head(h):
            kvh = h // 4
            ks = kvh % 2
            hs = h % 2
            if h % 4 == 0:
                S.dma('sp', kt[:, ks, :], g.KT[kvh], writes=[('kt', ks)])
                S.dma('act', vv[:, ks, :, :], g.Vs[kvh].rearrange("(i p) d -> p i d", p=128), writes=[('vv', ks)])
            S.dma('sp', qt[:, hs, :], g.QT[h], writes=[('qt', hs)])
            S.dma('act', gg[:, hs, :, :], gv[:, :, h * 128:(h + 1) * 128], writes=[('gg', hs)])

        def geom(n):
            h, i = divmod(n, 32)
            lo, hi = max(i - 1, 0), min(i + 1, 31)
            return h, i, lo, hi, (lo - i + 1) * 128, (hi - i + 2) * 128

        def st0(n):
            h, i, lo, hi, c0, c1 = geom(n)
            if i == 8 and h + 1 < 16:
                load_head(h + 1)
            s = n % 3
            hs, ks = h % 2, (h // 4) % 2
            for jb in range(lo, hi + 1):
                blk = jb - i + 1
                S.pe(lambda e, s=s, blk=blk, hs=hs, ks=ks, i=i, jb=jb: e.matmul(pss[:, s, blk * 128:(blk + 1) * 128], qt[:, hs, i * 128:(i + 1) * 128], kt[:, ks, jb * 128:(jb + 1) * 128], start=True, stop=True),
                     [('qt', hs), ('kt', ks)], [('pss', s)])

        def st1(n):
            h, i, lo, hi, c0, c1 = geom(n)
            s = n % 3
            q = n % 6
            S.dve(lambda e: e.scalar_tensor_tensor(lgt[:, s, c0:c1], pss[:, s, c0:c1], SCALE, ab[:, h, c0:c1], ALU.mult, ALU.add),
                  [('pss', s), 'ab'], [('lgt', s)])
            S.dve(lambda e: e.reduce_max(sc[:, q, 0:1], lgt[:, s, c0:c1], AX.X), [('lgt', s)], [('sc', q, 0)])
            S.dve(lambda e: e.tensor_scalar(sc[:, q, 2:3], sc[:, q, 0:1], sk[:, h:h + 1], -1.0, ALU.max, ALU.mult), [('sc', q, 0), 'sk'], [('sc', q, 0)])

        def st2(n):
            h, i, lo, hi, c0, c1 = geom(n)
            s = n % 3
            q = n % 6
            S.act(lambda e: e.activation(pn[:, s, c0:c1], lgt[:, s, c0:c1], AF.Exp, bias=sc[:, q, 2:3], accum_out=sc[:, q, 3:4]),
                  [('lgt', s), ('sc', q, 0)], [('pn', s), ('sc', q, 1)])
            S.act(lambda e: e.activation(sc[:, q, 4:5], sk[:, h:h + 1], AF.Exp, bias=sc[:, q, 2:3]), [('sc', q, 0), 'sk'], [('sc', q, 1)])

        def st3(n):
            h, i, lo, hi, c0, c1 = geom(n)
            s = n % 3
            q = n % 6
            S.dve(lambda e: e.tensor_tensor(sc[:, q, 5:6], sc[:, q, 3:4], sc[:, q, 4:5], ALU.add), [('sc', q, 1)], [('sc', q, 2)])
            S.dve(lambda e: e.reciprocal(sc[:, q, 6:7], sc[:, q, 5:6]), [('sc', q, 2)], [('sc', q, 2)])

        def st4(n):
            h, i, lo, hi, c0, c1 = geom(n)
            s = n % 3
            t2 = n % 2
            for jb in range(lo, hi + 1):
                blk = jb - i + 1
                S.pe(lambda e, blk=blk: e.transpose(ptt[:, t2, blk, :], pn[:, s, blk * 128:(blk + 1) * 128], idb[:]), [('pn', s), 'idb'], [('ptt', t2)])

        def st5(n):
            h, i, lo, hi, c0, c1 = geom(n)
            s = n % 3
            t2 = n % 2
            b0, b1 = lo - i + 1, hi - i + 2
            S.act(lambda e: e.copy(pT[:, s, b0:b1, :], ptt[:, t2, b0:b1, :]), [('ptt', t2)], [('pT', s)])

        def st6(n):
            h, i, lo, hi, c0, c1 = geom(n)
            s = n % 3
            ks = (h // 4) % 2
            for jb in range(lo, hi + 1):
                blk = jb - i + 1
                S.pe(lambda e, blk=blk, jb=jb: e.matmul(po[:, s, 0:128], pT[:, s, blk, :], vv[:, ks, jb, :], start=(jb == lo), stop=(jb == hi)),
                     [('pT', s), ('vv', ks)], [('po', s)])

        def st7(n):
            h, i, lo, hi, c0, c1 = geom(n)
            s = n % 3
            hs = h % 2
            q = n % 6
            S.dve(lambda e: e.scalar_tensor_tensor(yh[:, hs, i, :], po[:, s, 0:128], sc[:, q, 6:7], gg[:, hs, i, :], ALU.mult, ALU.mult),
                  [('po', s), ('gg', hs), ('sc', q, 2)], [('yh', hs)])
            if i == 31:
                S.dma('act', yv[:, :, h * 128:(h + 1) * 128], yh[:, hs, :, :], reads=[('yh', hs)])

        load_head(0)
        skew(512, [st0, st1, st2, st3, st4, st5, st6, st7])
        S.emit()


_DBG = []


def build(nslots=2, stop_after=None):
    nc = bass.Bass("TRN2", target_bir_lowering=False)
    g = K()

    def inp(name, shape, dt=F32):
        return nc.dram_tensor(name, list(shape), dt, kind="ExternalInput").ap()

    def scr(name, shape, dt):
        kind = "ExternalOutput" if name in _DBG else "Internal"
        return nc.dram_tensor(name, list(shape), dt, kind=kind).ap()

    g.x_in = inp("x_in", [nslots, L, D])
    g.w_hin = inp("w_hin", [D, 4 * D]); g.b_hin = inp("b_hin", [4 * D])
    g.w_sc = inp("w_sc", [3, 3 * D]); g.b_sc = inp("b_sc", [3 * D])
    g.w_f1 = inp("w_f1", [33, 64]); g.b_f1 = inp("b_f1", [64]); g.fr1 = inp("fr1", [64])
    g.w_f2 = inp("w_f2", [64, 64]); g.b_f2 = inp("b_f2", [64]); g.fr2 = inp("fr2", [64])
    g.w_f3 = inp("w_f3", [64, 64]); g.b_f3 = inp("b_f3", [64]); g.fr3 = inp("fr3", [64])
    g.w_f4 = inp("w_f4", [64, 4 * D]); g.h_bias = inp("h_bias", [2, D])
    g.w_hout = inp("w_hout", [D, D]); g.b_hout = inp("b_hout", [D])
    g.w_ain = inp("w_ain", [D, 5120]); g.sink = inp("sink", [16]); g.w_aout = inp("w_aout", [D, D])
    g.ln_g = inp("ln_g", [2, D]); g.ln_b = inp("ln_b", [2, D])
    g.ident_f = inp("ident_f", [128, 128]); g.ident_b = inp("ident_b", [128, 128], BF16)
    g.featT = inp("featT", [33, L]); g.tnorm = inp("tnorm", [128, 32]); g.negdelta = inp("negdelta", [D])
    g.F1c = inp("F1c", [64, 128, 128], BF16); g.Jrev = inp("Jrev", [128, 128], BF16); g.Cmat = inp("Cmat", [128, 4, 64], BF16)
    g.Emat = inp("Emat", [128, 3, 128], BF16); g.Tmat = inp("Tmat", [128, 128, 32], BF16)
    g.abias = inp("abias", [128, 16, 384]); g.hvec = inp("hvec", [128, 5, 48])
    g.y_out = nc.dram_tensor("y_out", [nslots, L, D], F32, kind="ExternalOutput").ap()

    g.wb_hin = scr("wb_hin", [D, 4 * D], BF16); g.wb_hout = scr("wb_hout", [D, D], BF16)
    g.wb_ain = scr("wb_ain", [D, 5120], BF16); g.wb_aout = scr("wb_aout", [D, D], BF16)
    g.H3 = scr("H3", [64, L], F32)
    g.Gt = scr("Gt", [2, 2 * L, D], BF16)
    g.nrm = scr("nrm", [2, D], F32)
    g.KF = scr("KF", [2, 32, 2, 128, D], F32)
    g.Ascr = scr("Ascr", [128, 128, D], BF16)
    g.Bscr = scr("Bscr", [128, 128, D], BF16)
    g.Ptm = scr("Ptm", [L + 2, 4 * D], F32)
    g.U0 = scr("U0", [L, D], F32); g.U1 = scr("U1", [L, D], F32); g.U2 = scr("U2", [L, D], F32)
    g.Z1 = scr("Z1", [L, D], F32)
    g.zT = scr("zT", [L, D], BF16)
    g.Ytm = scr("Ytm", [L, D], BF16)
    g.X1 = scr("X1", [L, D], F32)
    g.QT = scr("QT", [16, 128, L], BF16); g.KT = scr("KT", [4, 128, L], BF16)
    g.Vs = scr("Vs", [4, L, 128], BF16); g.Gtm = scr("Gtm", [L, D], F32)

    ctx = SemCtx()

    def done(tag):
        return stop_after is not None and stop_after == tag

    ph_cast(nc, ctx, [(g.w_hin, g.wb_hin, D, 4 * D), (g.w_hout, g.wb_hout, D, D),
                      (g.w_ain, g.wb_ain, D, 5120), (g.w_aout, g.wb_aout, D, D)])
    ph_zero_rows(nc, ctx, g)
    ph_filter_mlp(nc, ctx, g)
    ph_filter_gen(nc, ctx, g)
    if done('fgen'):
        return nc
    for o in range(2):
        ph_fft_s1(nc, ctx, g, [(g.Gt[o], g.Ascr)])
        ph_filter_s2(nc, ctx, g, o)
    if done('filter'):
        return nc
    for slot in range(nslots):
        x0 = g.x_in[slot]
        ph_inproj_hy2(nc, ctx, g, x0)
        if done('sconv'):
            return nc
        for o in range(2):
            ph_fft_s1(nc, ctx, g, [(g.zT, g.Ascr)])
            ph_conv_mid(nc, ctx, g, o)
            ph_conv_is2(nc, ctx, g, o)
            if done('conv%d' % o):
                return nc
        ph_outproj(nc, ctx, g, g.Ytm, x0, g.wb_hout, g.b_hout, g.ln_g[0], g.ln_b[0], g.X1)
        if done('hy'):
            return nc
        ph_inproj(nc, ctx, g, g.X1, g.wb_ain, 5120, 'at')
        ph_attn(nc, ctx, g)
        ph_outproj(nc, ctx, g, g.Ytm, g.X1, g.wb_aout, None, g.ln_g[1], g.ln_b[1], g.y_out[slot])
    return nc


def make_in_maps(inputs, nslots=2):
    c = _consts()
    f32 = lambda a: np.ascontiguousarray(np.asarray(a, dtype=np.float32))
    shared = {
        "w_hin": f32(inputs["hy_w_in"][0]), "b_hin": f32(inputs["hy_b_in"][0]),
        "w_sc": f32(inputs["hy_w_sc"][0]), "b_sc": f32(inputs["hy_b_sc"][0]),
        "w_f1": f32(inputs["hy_w_f1"][0]), "b_f1": f32(inputs["hy_b_f1"][0]), "fr1": f32(inputs["hy_fr1"][0]),
        "w_f2": f32(inputs["hy_w_f2"][0]), "b_f2": f32(inputs["hy_b_f2"][0]), "fr2": f32(inputs["hy_fr2"][0]),
        "w_f3": f32(inputs["hy_w_f3"][0]), "b_f3": f32(inputs["hy_b_f3"][0]), "fr3": f32(inputs["hy_fr3"][0]),
        "w_f4": f32(inputs["hy_w_f4"][0]), "h_bias": f32(inputs["hy_h_bias"][0]),
        "w_hout": f32(inputs["hy_w_out"][0]), "b_hout": f32(inputs["hy_b_out"][0]),
        "w_ain": f32(inputs["at_w_in"][0]), "sink": f32(inputs["at_sink"][0]), "w_aout": f32(inputs["at_w_out"][0]),
        "ln_g": f32(inputs["ln_g"]), "ln_b": f32(inputs["ln_b"]),
    }
    shared.update(c)
    hv = np.stack([shared["w_sc"][0], shared["w_sc"][1], shared["w_sc"][2], shared["b_sc"], shared["b_hin"][:3 * D]], axis=0)
    shared["hvec"] = np.ascontiguousarray(hv.reshape(5, 48, 128).transpose(2, 0, 1))
    xs = f32(inputs["x_sample"])
    xp = f32(inputs["x_prompt"])
    maps = []
    for core in range(8):
        second = xp[core] if core < 2 else xs[core]
        m = dict(shared)
        m["x_in"] = np.ascontiguousarray(np.stack([xs[core], second], axis=0)[:nslots])
        maps.append(m)
    return maps


def kernel(**inputs):
    nc = build()
    maps = make_in_maps(inputs)
    res = run_bass_kernel_spmd(nc, maps, core_ids=list(range(8)))
    ys = np.stack([res.results[c]["y_out"][0] for c in range(8)], axis=0).astype(np.float32)
    yp = np.stack([res.results[c]["y_out"][1] for c in range(2)], axis=0).astype(np.float32)
    return (yp, ys)
```
